# Optimizing a Trainium2 kernel written in Bass

```python
import math
import jax, jax.numpy as jnp
from jax import lax
import numpy as np

D_MODEL = 1024
BATCH = 32
SEQ = 2048
DEPTH = 4
DEC_BATCH = 8
DEC_SEQ = 32
PAST_LEN = 1024

CHUNK = 64
N_A = DEPTH // 2
N_B = DEPTH - N_A
POOL_WINDOWS = (2, 4, 8, 16)
N_POOL_GROUPS = len(POOL_WINDOWS)
POOL_GROUP = D_MODEL // N_POOL_GROUPS
POOL_STATE = max(POOL_WINDOWS) - 1
N_HEADS = D_MODEL // 128
QK_NOPE = 128
QK_ROPE = 64
V_HEAD = 128
KV_LORA = D_MODEL // 4
Q_LORA = 3 * D_MODEL // 8
D_FF = ((8 * D_MODEL // 3 + 127) // 128) * 128
ROPE_BASE = 10000.0
Q_BLOCK = 128
EPS = 1e-6
ATTN_SCALE = (QK_NOPE + QK_ROPE) ** -0.5

kernel_name = "yoco_pool_mla_streaming_step"


def rms_norm(x, g):
    xf = x.astype(jnp.float32)
    y = xf * lax.rsqrt(jnp.mean(xf * xf, axis=-1, keepdims=True) + EPS)
    return (y * g.astype(jnp.float32)).astype(x.dtype)


def swiglu(h, w_in, w_out):
    gate, up = jnp.split(h @ w_in, 2, axis=-1)
    return (jax.nn.silu(gate) * up) @ w_out


def rope(x, pos):
    half = x.shape[-1] // 2
    inv = ROPE_BASE ** (-(jnp.arange(0, x.shape[-1], 2, dtype=jnp.float32) / x.shape[-1]))
    ang = pos.astype(jnp.float32)[:, None] * inv[None, :]
    cos = jnp.cos(ang)[None, :, None, :]
    sin = jnp.sin(ang)[None, :, None, :]
    xf = x.astype(jnp.float32)
    x1, x2 = xf[..., :half], xf[..., half:]
    return jnp.concatenate([x1 * cos - x2 * sin, x2 * cos + x1 * sin], axis=-1).astype(x.dtype)


def pool_mix(h, prev, start, w_pool, scale):
    B, S, _ = h.shape
    ext = jnp.concatenate([prev.astype(h.dtype), h], axis=1)
    pos_ext = start - POOL_STATE + jnp.arange(POOL_STATE + S)
    extf = jnp.where((pos_ext >= 0)[None, :, None], ext.astype(jnp.float32), 0.0)
    cs = jnp.concatenate([jnp.zeros((B, 1, D_MODEL), jnp.float32), jnp.cumsum(extf, axis=1)], axis=1)
    pos = start + jnp.arange(S)
    end = cs[:, POOL_STATE + 1:]
    hf = h.astype(jnp.float32)
    diffs = []
    for g, w in enumerate(POOL_WINDOWS):
        lo, hi = g * POOL_GROUP, (g + 1) * POOL_GROUP
        begin = cs[:, POOL_STATE + 1 - w:POOL_STATE + 1 - w + S, lo:hi]
        cnt = jnp.minimum(pos + 1, w).astype(jnp.float32)[None, :, None]
        diffs.append((end[..., lo:hi] - begin) / cnt - hf[..., lo:hi])
    d = jnp.stack(diffs, axis=2).astype(h.dtype)
    out = jnp.einsum('bsgc,gcd->bsgd', d, w_pool).reshape(B, S, D_MODEL)
    return out * scale, ext[:, -POOL_STATE:]


def mla_shared_latent(x, pos, kv_norm, w_dkv, c_norm, w_kr, kr_norm):
    h = rms_norm(x, kv_norm)
    c = rms_norm(h @ w_dkv, c_norm)
    kr = rope(rms_norm(h @ w_kr, kr_norm)[:, :, None, :], pos)[:, :, 0]
    return c, kr


def mla_shared_kv(c_all, w_uk, kn_norm, w_uv):
    B, T, _ = c_all.shape
    k_nope = rms_norm((c_all @ w_uk).reshape(B, T, N_HEADS, QK_NOPE), kn_norm)
    v = (c_all @ w_uv).reshape(B, T, N_HEADS, V_HEAD)
    return k_nope, v


def mla_attend(h, pos, start, P, k_nope, k_rope, v, w_dq, q_lat_norm, w_uq, qn_norm, qr_norm, w_o):
    B, S, _ = h.shape
    q_lat = rms_norm(h @ w_dq, q_lat_norm)
    q = (q_lat @ w_uq).reshape(B, S, N_HEADS, QK_NOPE + QK_ROPE)
    q_nope = rms_norm(q[..., :QK_NOPE], qn_norm)
    q_rope = rope(rms_norm(q[..., QK_NOPE:], qr_norm), pos)
    outs = []
    for qb in range(-(-S // Q_BLOCK)):
        q0 = qb * Q_BLOCK
        q1 = min(S, q0 + Q_BLOCK)
        nk = P + q1
        s = (jnp.einsum('bqhd,bkhd->bhqk', q_nope[:, q0:q1], k_nope[:, :nk], preferred_element_type=jnp.float32)
             + jnp.einsum('bqhr,bkr->bhqk', q_rope[:, q0:q1], k_rope[:, :nk], preferred_element_type=jnp.float32)) * ATTN_SCALE
        qc = (start + jnp.arange(q0, q1)) // CHUNK
        kc = (start - P + jnp.arange(nk)) // CHUNK
        s = jnp.where((kc[None, :] <= qc[:, None])[None, None], s, -jnp.inf)
        p = jax.nn.softmax(s, axis=-1)
        outs.append(jnp.einsum('bhqk,bkhd->bqhd', p.astype(v.dtype), v[:, :nk]))
    o = jnp.concatenate(outs, axis=1).reshape(B, S, N_HEADS * V_HEAD)
    return o @ w_o


def trunk(x, pool_prev, ckv_past, kr_past, start,
          ffn1_norm, ffn1_w_in, ffn1_w_out, mix_norm, ffn2_norm, ffn2_w_in, ffn2_w_out,
          pool_w, pool_scale, kv_norm, w_dkv, c_norm, w_kr, kr_norm, w_uk, kn_norm, w_uv,
          w_dq, q_lat_norm, w_uq, qn_norm, qr_norm, w_o):
    B, S, _ = x.shape
    P = ckv_past.shape[1]
    pos = start + jnp.arange(S, dtype=jnp.int32)
    new_pool = []
    c_new = kr_new = k_nope = k_rope = v = None
    for layer in range(DEPTH):
        if layer == N_A:
            c_new, kr_new = mla_shared_latent(x, pos, kv_norm, w_dkv, c_norm, w_kr, kr_norm)
            c_all = jnp.concatenate([ckv_past.astype(c_new.dtype), c_new], axis=1)
            k_rope = jnp.concatenate([kr_past.astype(kr_new.dtype), kr_new], axis=1)
            k_nope, v = mla_shared_kv(c_all, w_uk, kn_norm, w_uv)
        x = x + 0.5 * swiglu(rms_norm(x, ffn1_norm[layer]), ffn1_w_in[layer], ffn1_w_out[layer])
        h = rms_norm(x, mix_norm[layer])
        if layer < N_A:
            m, st = pool_mix(h, pool_prev[layer], start, pool_w[layer], pool_scale[layer])
            new_pool.append(st)
        else:
            i = layer - N_A
            m = mla_attend(h, pos, start, P, k_nope, k_rope, v, w_dq[i], q_lat_norm[i], w_uq[i],
                           qn_norm[i], qr_norm[i], w_o[i])
        x = x + m
        x = x + 0.5 * swiglu(rms_norm(x, ffn2_norm[layer]), ffn2_w_in[layer], ffn2_w_out[layer])
    return x, jnp.stack(new_pool, axis=0), c_new, kr_new


def setup_inputs(seed: int = 0) -> dict:
    key = jax.random.key(seed)
    ks = jax.random.split(key, 40)
    f32 = jnp.float32
    nrm = lambda k, shape, s: jax.random.normal(k, shape, f32) * s
    gain = lambda k, shape: 1.0 + 0.01 * jax.random.normal(k, shape, f32)
    HQ = N_HEADS * (QK_NOPE + QK_ROPE)
    return {
        'x_prompt': nrm(ks[0], (BATCH, SEQ, D_MODEL), 1.0),
        'x_sample': nrm(ks[1], (DEC_BATCH, DEC_SEQ, D_MODEL), 1.0),
        'state_pool': nrm(ks[2], (N_A, DEC_BATCH, POOL_STATE, D_MODEL), 1.0),
        'cache_ckv': nrm(ks[3], (DEC_BATCH, PAST_LEN, KV_LORA), 1.0),
        'cache_krope': nrm(ks[4], (DEC_BATCH, PAST_LEN, QK_ROPE), 1.0),
        'ffn1_norm': gain(ks[5], (DEPTH, D_MODEL)),
        'ffn1_w_in': nrm(ks[6], (DEPTH, D_MODEL, 2 * D_FF), D_MODEL ** -0.5),
        'ffn1_w_out': nrm(ks[7], (DEPTH, D_FF, D_MODEL), D_FF ** -0.5),
        'mix_norm': gain(ks[8], (DEPTH, D_MODEL)),
        'ffn2_norm': gain(ks[9], (DEPTH, D_MODEL)),
        'ffn2_w_in': nrm(ks[10], (DEPTH, D_MODEL, 2 * D_FF), D_MODEL ** -0.5),
        'ffn2_w_out': nrm(ks[11], (DEPTH, D_FF, D_MODEL), D_FF ** -0.5),
        'pool_w': nrm(ks[12], (N_A, N_POOL_GROUPS, POOL_GROUP, POOL_GROUP), POOL_GROUP ** -0.5),
        'pool_scale': 0.5 + 0.05 * jax.random.normal(ks[13], (N_A, D_MODEL), f32),
        'kv_norm': gain(ks[14], (D_MODEL,)),
        'w_dkv': nrm(ks[15], (D_MODEL, KV_LORA), D_MODEL ** -0.5),
        'c_norm': gain(ks[16], (KV_LORA,)),
        'w_kr': nrm(ks[17], (D_MODEL, QK_ROPE), D_MODEL ** -0.5),
        'kr_norm': gain(ks[18], (QK_ROPE,)),
        'w_uk': nrm(ks[19], (KV_LORA, N_HEADS * QK_NOPE), KV_LORA ** -0.5),
        'kn_norm': gain(ks[20], (QK_NOPE,)),
        'w_uv': nrm(ks[21], (KV_LORA, N_HEADS * V_HEAD), KV_LORA ** -0.5),
        'w_dq': nrm(ks[22], (N_B, D_MODEL, Q_LORA), D_MODEL ** -0.5),
        'q_lat_norm': gain(ks[23], (N_B, Q_LORA)),
        'w_uq': nrm(ks[24], (N_B, Q_LORA, HQ), Q_LORA ** -0.5),
        'qn_norm': gain(ks[25], (N_B, QK_NOPE)),
        'qr_norm': gain(ks[26], (N_B, QK_ROPE)),
        'w_o': nrm(ks[27], (N_B, N_HEADS * V_HEAD, D_MODEL), (N_HEADS * V_HEAD) ** -0.5),
    }


def reference(x_prompt, x_sample, state_pool, cache_ckv, cache_krope,
              ffn1_norm, ffn1_w_in, ffn1_w_out, mix_norm, ffn2_norm, ffn2_w_in, ffn2_w_out,
              pool_w, pool_scale, kv_norm, w_dkv, c_norm, w_kr, kr_norm, w_uk, kn_norm, w_uv,
              w_dq, q_lat_norm, w_uq, qn_norm, qr_norm, w_o):
    Bp = x_prompt.shape[0]
    pool_prev_p = jnp.zeros((N_A, Bp, POOL_STATE, D_MODEL), x_prompt.dtype)
    ckv_prev_p = jnp.zeros((Bp, 0, KV_LORA), x_prompt.dtype)
    kr_prev_p = jnp.zeros((Bp, 0, QK_ROPE), x_prompt.dtype)
    y_prompt, new_pool_prompt, new_ckv_prompt, new_krope_prompt = trunk(
        x_prompt, pool_prev_p, ckv_prev_p, kr_prev_p, 0,
        ffn1_norm, ffn1_w_in, ffn1_w_out, mix_norm, ffn2_norm, ffn2_w_in, ffn2_w_out,
        pool_w, pool_scale, kv_norm, w_dkv, c_norm, w_kr, kr_norm, w_uk, kn_norm, w_uv,
        w_dq, q_lat_norm, w_uq, qn_norm, qr_norm, w_o)
    y_sample, new_pool_sample, new_ckv_sample, new_krope_sample = trunk(
        x_sample, state_pool, cache_ckv, cache_krope, cache_ckv.shape[1],
        ffn1_norm, ffn1_w_in, ffn1_w_out, mix_norm, ffn2_norm, ffn2_w_in, ffn2_w_out,
        pool_w, pool_scale, kv_norm, w_dkv, c_norm, w_kr, kr_norm, w_uk, kn_norm, w_uv,
        w_dq, q_lat_norm, w_uq, qn_norm, qr_norm, w_o)
    return (y_prompt, y_sample, new_pool_prompt, new_pool_sample,
            new_ckv_prompt, new_krope_prompt, new_ckv_sample, new_krope_sample)
```

```python
import contextlib
import numpy as np
import concourse.bass as bass
import concourse.mybir as mybir
from concourse.bass_utils import run_bass_kernel_spmd

F32 = mybir.dt.float32
BF16 = mybir.dt.bfloat16
AF = mybir.ActivationFunctionType
ALU = mybir.AluOpType

D = 1024
NCH = 8
DFF = 2816
NFC = 22
NH = 8
KVL = 256
QL = 384
ROPE = 64
DEPTH = 4
N_A = 2
PST = 15
EPS = 1e-6
ATTN_SCALE = float((128 + 64) ** -0.5)
WINDOWS = (2, 2, 4, 4, 8, 8, 16, 16)
TT = 512
N_CORES = 8

G_FFN1, G_MIX, G_FFN2, G_PSC, G_KV, G_C, G_QL, G_KN, G_QN, G_KR, G_QR, NG = 0, 32, 64, 96, 112, 120, 122, 128, 129, 131, 132, 134


class Sem:
    def __init__(self, h):
        self.h = h
        self.n = 0


class Cell:
    __slots__ = ("w", "r")

    def __init__(self):
        self.w = None
        self.r = []


def cells(n):
    return [Cell() for _ in range(n)]


class Eng:
    def __init__(self, name, sem, is_pe=False):
        self.name = name
        self.sem = sem
        self.ops = []
        self.waited = {}
        self.is_pe = is_pe


class Prog:
    def __init__(self):
        self.eng = {}

    def _flat(self, xs):
        out = []
        for x in xs:
            if isinstance(x, Cell):
                out.append(x)
            else:
                out.extend(self._flat(x))
        return out

    def _deps(self, e, reads, writes):
        deps = {}

        def add(t, war):
            if t is None:
                return
            s, v = t
            if s is e.sem:
                if e.is_pe:
                    return
            if deps.get(s, 0) < v:
                deps[s] = v

        for c in reads:
            add(c.w, False)
        for c in writes:
            add(c.w, False)
            for t in c.r:
                add(t, True)
        for s, v in deps.items():
            if e.waited.get(s, 0) < v:
                e.waited[s] = v
                e.ops.append(("w", s.h, v))

    def op(self, ename, fns, reads=(), writes=(), sem=None, inc=1):
        e = self.eng[ename]
        reads = self._flat(reads)
        writes = self._flat(writes)
        self._deps(e, reads, writes)
        if not isinstance(fns, (list, tuple)):
            fns = [fns]
        s = sem if sem is not None else e.sem
        if sem is not None:
            for f in fns:
                s.n += inc
                e.ops.append(("i", f, s.h, inc))
        else:
            for f in fns[:-1]:
                e.ops.append(("i", f, None, 0))
            s.n += inc
            e.ops.append(("i", fns[-1], s.h, inc))
        t = (s, s.n)
        for c in reads:
            c.r.append(t)
        for c in writes:
            c.w = t
            c.r = []
        return t

    def wait_ticket(self, ename, t):
        e = self.eng[ename]
        s, v = t
        if e.waited.get(s, 0) < v:
            e.waited[s] = v
            e.ops.append(("w", s.h, v))

    def replay(self, ename, hw):
        for o in self.eng[ename].ops:
            if o[0] == "w":
                hw.wait_ge(o[1], o[2])
            else:
                ins = o[1](hw)
                if o[2] is not None:
                    ins.then_inc(o[2], o[3])


class PsumPool:
    def __init__(self, tensors):
        self.t = tensors
        self.cells = [Cell() for _ in tensors]
        self.free = list(range(len(tensors)))

    def alloc(self):
        i = self.free.pop(0)
        return i

    def release(self, i):
        self.free.append(i)


def build(NSEQ, SEQ, with_sample=True, PAST=1024, SNT=32):
    import os
    SKIP = set(os.environ.get('KSKIP', '').split(','))
    MAXL = int(os.environ.get('KMAXL', '4'))
    NROPE = int(os.environ.get('KNROPE', '8'))
    nc = bass.Bass("TRN2", target_bir_lowering=False)
    NPT = NSEQ * SEQ
    TPS = SEQ // TT

    def din(name, shape, dt=F32):
        return nc.dram_tensor(name, list(shape), dt, kind="ExternalInput").ap()

    def dout(name, shape, dt=F32):
        return nc.dram_tensor(name, list(shape), dt, kind="ExternalOutput").ap()

    xp = din("xp", [NPT, D])
    xs = din("xs", [SNT, D])
    spool = din("spool", [N_A, PST, D])
    cckv = din("cckv", [PAST, KVL])
    ckr = din("ckr", [PAST, ROPE])
    w1in = din("w1in", [DEPTH, D, 2 * DFF])
    w1out = din("w1out", [DEPTH, DFF, D])
    w2in = din("w2in", [DEPTH, D, 2 * DFF])
    w2out = din("w2out", [DEPTH, DFF, D])
    poolw = din("poolw", [N_A, 4, 256, 256])
    wdkv = din("wdkv", [D, KVL])
    wkr = din("wkr", [D, ROPE])
    wuk = din("wuk", [KVL, D])
    wuv = din("wuv", [KVL, D])
    wdq = din("wdq", [2, D, QL])
    wuq = din("wuq", [2, QL, NH * 192])
    wo = din("wo", [2, D, D])
    gains_d = din("gains", [128, NG])
    ident_d = din("ident", [128, 128])
    rrot_d = din("rrot", [64, 64])
    cos_d = din("cosT", [64, 2048])
    sin_d = din("sinT", [64, 2048])
    invc_d = din("invcnt", [128, 64])

    yp = dout("yp", [NPT, D])
    ys = dout("ys", [SNT, D])
    npp = dout("npp", [N_A, NSEQ, PST, D])
    nps = dout("nps", [N_A, PST, D])
    ckvp = dout("ckvp", [NPT, KVL])
    krp = dout("krp", [NPT, ROPE])
    ckvs = dout("ckvs", [SNT, KVL])
    krs = dout("krs", [SNT, ROPE])

    kn_s = nc.dram_tensor("kn_s", [NH, 128, 2048], BF16, kind="Internal").ap()
    v_s = nc.dram_tensor("v_s", [NH, 128, 16, 128], BF16, kind="Internal").ap()
    kn_s_cells = cells(NH)
    v_s_cells = cells(16)

    P = Prog()
    with contextlib.ExitStack() as es:
        def sb(name, shape, dt):
            return es.enter_context(nc.sbuf_tensor("sb_" + name, list(shape), dt))

        def newsem(name):
            return Sem(es.enter_context(nc.semaphore(name)))

        P.eng["pe"] = Eng("pe", newsem("s_pe"), is_pe=True)
        P.eng["act"] = Eng("act", newsem("s_act"))
        P.eng["dve"] = Eng("dve", newsem("s_dve"))
        P.eng["pool"] = Eng("pool", newsem("s_pool"))
        P.eng["sp"] = Eng("sp", newsem("s_sp"))

        xT = sb("xT", [128, NCH, TT], F32)
        xc = cells(NCH)
        hb = sb("hb", [128, NCH, TT], BF16)
        hc = cells(NCH)
        act = sb("act", [128, NFC * TT], BF16)
        ac = cells(NFC)
        R_A, R_B = 3, 4
        win = [sb(f"win{i}", [128, NCH, 2, 256], BF16) for i in range(R_A)]
        win_c = [cells(1) for _ in range(R_A)]
        win_sem = [newsem(f"s_win{i}") for i in range(R_A)]
        wout = [sb(f"wout{i}", [128, NFC, 128], BF16) for i in range(R_B)]
        wout_c = [cells(1) for _ in range(R_B)]
        wout_sem = [newsem(f"s_wout{i}") for i in range(R_B)]
        WM = sb("WM", [128, 15872], BF16)
        wm_c = cells(4)
        wm_sems = [newsem(f"s_wm{i}") for i in range(4)]
        kvk = [sb(f"kvk{i}", [128, 2048], BF16) for i in range(2)]
        kvv = [sb(f"kvv{i}", [128, 16, 128], BF16) for i in range(2)]
        kv_c = [cells(1) for _ in range(2)]
        kv_sem = [newsem(f"s_kv{i}") for i in range(2)]
        Kr = sb("Kr", [64, 2048], BF16)
        kr_c = cells(1)
        pT = [sb(f"pT{i}", [128, TT], BF16) for i in range(4)]
        pT_c = [cells(1) for _ in range(4)]
        xio = [sb(f"xio{i}", [128, D], F32) for i in range(2)]
        xio_c = [cells(1) for _ in range(2)]
        xio_sem = [newsem(f"s_xio{i}") for i in range(2)]
        rstd_r = [sb(f"rstd{i}", [128, TT], F32) for i in range(2)]
        rstd_rc = [cells(1) for _ in range(2)]
        nrm_n = [0]
        sgl = [sb(f"sgl{i}", [128, TT], F32) for i in range(2)]
        sgl_c = [cells(1) for _ in range(2)]
        usl = [sb(f"usl{i}", [128, TT], F32) for i in range(2)]
        usl_c = [cells(1) for _ in range(2)]
        sqx = sb("sqx", [128, NCH, TT], BF16)
        sqx_c = cells(NCH)
        xsq = [False] * NCH
        ident = sb("ident", [128, 128], F32)
        ones = sb("ones", [128, 128], BF16)
        rrot = sb("rrot", [64, 64], F32)
        gains = sb("gains", [128, NG], F32)
        invc = sb("invc", [128, 64], F32)
        const_c = cells(1)
        const_sem = newsem("s_const")
        cosS = sb("cosS", [64, TT], F32)
        sinS = sb("sinS", [64, TT], F32)
        cs_c = cells(1)
        cs_sem = newsem("s_cs")
        st = [sb(f"st{l}", [128, NCH, PST], F32) for l in range(N_A)]
        st_c = [cells(1) for _ in range(N_A)]
        osm = [sb(f"osm{i}", [128, 320], F32) for i in range(2)]
        osm_c = [cells(1) for _ in range(2)]
        osm_sem = [newsem(f"s_osm{i}") for i in range(2)]
        rp_a = sb("rp_a", [64, TT], F32)
        rp_b = sb("rp_b", [64, TT], F32)
        rp_c = cells(2)
        rp_n = [sb(f"rp_n{i}", [64, TT], F32) for i in range(2)]
        rp_nc = [cells(1) for _ in range(2)]
        rp_hi = [sb(f"rp_hi{i}", [64, TT], BF16) for i in range(2)]
        rp_lo = [sb(f"rp_lo{i}", [64, TT], BF16) for i in range(2)]
        rp_hc = [cells(2) for _ in range(2)]
        rrotB = sb("rrotB", [64, 64], BF16)
        rrotB_c = cells(1)
        misc_sem = newsem("s_misc")
        scr_sem = newsem("s_scr")

        def av(byte_off, free_shape, dt):
            n = int(np.prod(free_shape))
            if dt == BF16:
                a_ = act[:, byte_off // 2:byte_off // 2 + n]
            else:
                a_ = act[:, byte_off // 2:byte_off // 2 + 2 * n].bitcast(F32)
            if len(free_shape) == 2:
                a_ = a_.rearrange("p (a b) -> p a b", a=free_shape[0])
            return a_

        def acr(byte_off, nbytes):
            return ac[byte_off // 1024:(byte_off + nbytes + 1023) // 1024]

        def sqv(c):
            return av((21 - c) * 1024, [TT], BF16)

        def sqc(c):
            return ac[21 - c]

        PS = PsumPool([es.enter_context(nc.psum_tensor(f"ps{i}", [128, TT], F32)) for i in range(8)])

        block = es.enter_context(nc.Block())

        def pe_mm(out, lhsT, rhs, start, stop):
            return lambda e: e.matmul(out, lhsT, rhs, start=start, stop=stop)

        def gcol(col, np_=128):
            return gains[0:np_, col:col + 1]

        P.op("sp", [lambda e: e.dma_start(out=ident[:], in_=ident_d),
                    lambda e: e.dma_start(out=rrot[:], in_=rrot_d),
                    lambda e: e.dma_start(out=gains[:], in_=gains_d),
                    lambda e: e.dma_start(out=invc[:], in_=invc_d)],
             writes=[const_c], sem=const_sem, inc=16)
        ones_c = cells(1)
        P.op("dve", lambda e: e.memset(ones[:], 1.0), writes=[ones_c])
        P.op("dve", lambda e: e.tensor_copy(out=rrotB[:], in_=rrot[:]), reads=[const_c], writes=[rrotB_c])

        win_list = []
        wout_list = []
        tiles = []
        for s in range(NSEQ):
            for t in range(TPS):
                tiles.append(("p", s, t))
        if with_sample:
            tiles.append(("s", 0, 0))
        for _ in tiles:
            for L in range(DEPTH):
                for which in (0, 1):
                    for fg in range(NFC // 2):
                        win_list.append((L, which, fg))
                    for dg in range(8):
                        wout_list.append((L, which, dg))
        win_issued = [0]
        wout_issued = [0]

        def issue_win(upto):
            while win_issued[0] <= upto and win_issued[0] < len(win_list):
                i = win_issued[0]
                L, which, fg = win_list[i]
                w = (w1in if which == 0 else w2in)[L].rearrange("(k p) f -> p k f", p=128)
                slot = i % R_A
                fns = []
                for g in (0, 1):
                    c0 = g * DFF + fg * 256
                    fns.append(lambda e, slot=slot, g=g, c0=c0, w=w: e.dma_start(out=win[slot][:, :, g, :], in_=w[:, :, c0:c0 + 256]))
                P.op("pool", fns, writes=[win_c[slot]], sem=win_sem[slot], inc=16)
                win_issued[0] += 1

        def issue_wout(upto):
            while wout_issued[0] <= upto and wout_issued[0] < len(wout_list):
                i = wout_issued[0]
                L, which, dg = wout_list[i]
                w = (w1out if which == 0 else w2out)[L].rearrange("(f p) d -> p f d", p=128)
                slot = i % R_B
                P.op("pool", lambda e, slot=slot, dg=dg, w=w: e.dma_start(out=wout[slot][:], in_=w[:, :, dg * 128:(dg + 1) * 128]),
                     writes=[wout_c[slot]], sem=wout_sem[slot], inc=16)
                wout_issued[0] += 1

        def wm_view(off, k, f):
            return WM[:, off:off + k * f].rearrange("p (k f) -> p k f", k=k)

        def load_wm(kind, idx):
            fns = []
            if kind == "pool":
                for g in range(4):
                    src_ = poolw[idx, g].rearrange("(k p) d -> p k d", p=128)
                    fns.append(lambda e, g=g, src_=src_: e.dma_start(out=wm_view(g * 512, 2, 256), in_=src_))
            elif kind == "lat":
                fns.append(lambda e: e.dma_start(out=wm_view(0, 8, 256), in_=wdkv.rearrange("(k p) f -> p k f", p=128)))
                fns.append(lambda e: e.dma_start(out=wm_view(2048, 8, 64), in_=wkr.rearrange("(k p) f -> p k f", p=128)))
                fns.append(lambda e: e.dma_start(out=wm_view(2560, 2, 1024), in_=wuk.rearrange("(k p) f -> p k f", p=128)))
                fns.append(lambda e: e.dma_start(out=wm_view(4608, 2, 1024), in_=wuv.rearrange("(k p) f -> p k f", p=128)))
            else:
                fns.append(lambda e: e.dma_start(out=wm_view(0, 8, 384), in_=wdq[idx].rearrange("(k p) f -> p k f", p=128)))
                fns.append(lambda e: e.dma_start(out=wm_view(3072, 3, 1536), in_=wuq[idx].rearrange("(k p) f -> p k f", p=128)))
                wo_r = wo[idx].rearrange("(k p) f -> p k f", p=128)
                fns.append(lambda e: e.dma_start(out=wm_view(7680, 8, 1024)[:, 0:4, :], in_=wo_r[:, 0:4, :]))
                fns.append(lambda e: e.dma_start(out=wm_view(7680, 8, 1024)[:, 4:8, :], in_=wo_r[:, 4:8, :]))
            return fns

        wm_pending = []
        wm_piece = [0]

        def wm_pop(n=1):
            for _ in range(n):
                if wm_pending:
                    pc = wm_piece[0] % 4
                    wm_piece[0] += 1
                    P.op("pool", [wm_pending.pop(0)], writes=[wm_c[pc]], sem=wm_sems[pc], inc=16)

        wm_users = []
        for _ in tiles:
            wm_users += [("pool", 0), ("pool", 1), ("lat", 0), ("attn", 0), ("attn", 1)]
        wm_ptr = [0]

        def next_wm(now=False):
            if wm_ptr[0] < len(wm_users):
                wm_pending.extend(load_wm(*wm_users[wm_ptr[0]]))
                wm_ptr[0] += 1
            if now:
                wm_pop(len(wm_pending))

        def x_square(c, NT):
            P.op("act", lambda e, c=c: e.activation(out=sqx[:, c, 0:NT], in_=xT[:, c, 0:NT], func=AF.Square),
                 reads=[xc[c]], writes=[sqx_c[c]])
            xsq[c] = True

        def norm_stats(src_fn, src_cells, nchunks, npart, dim, NT, is_x=False):
            if is_x:
                for c in range(nchunks):
                    if not xsq[c]:
                        x_square(c, NT)
                sq_ap = lambda c: sqx[:, c, 0:NT]
                sq_cl = lambda c: sqx_c[c]
            else:
                for c in range(nchunks):
                    src_ap = src_fn(c)
                    P.op("act", lambda e, c=c, src_ap=src_ap: e.activation(out=sqv(c)[0:npart, 0:NT], in_=src_ap, func=AF.Square),
                         reads=[src_cells[c]], writes=[sqc(c)])
                sq_ap = lambda c: sqv(c)[0:npart, 0:NT]
                sq_cl = sqc
            b = PS.alloc()
            P.op("pe", [pe_mm(PS.t[b][0:npart, 0:NT], ones[0:npart, 0:npart], sq_ap(c), c == 0, c == nchunks - 1)
                        for c in range(nchunks)],
                 reads=[[sq_cl(c) for c in range(nchunks)], ones_c], writes=[PS.cells[b]])
            ri = nrm_n[0] % 2
            nrm_n[0] += 1
            rstd, rstd_c = rstd_r[ri], rstd_rc[ri]
            P.op("act", lambda e: e.activation(out=rstd[0:npart, 0:NT], in_=PS.t[b][0:npart, 0:NT], func=AF.Ln,
                                               bias=eps_t[0:npart, 0:1], scale=1.0 / dim),
                 reads=[PS.cells[b], eps_c], writes=[rstd_c])
            PS.release(b)
            P.op("act", lambda e: e.activation(out=rstd[0:npart, 0:NT], in_=rstd[0:npart, 0:NT], func=AF.Exp, scale=-0.5),
                 reads=[rstd_c], writes=[rstd_c])
            return rstd, rstd_c

        eps_t = sb("eps_t", [128, 1], F32)
        eps_c = cells(1)
        P.op("dve", lambda e: e.memset(eps_t[:], EPS), writes=[eps_c])

        def x_norm_to_hb(gbase, NT):
            rstd, rstd_c = norm_stats(None, None, NCH, 128, D, NT, is_x=True)
            for c in range(NCH):
                P.op("dve", lambda e, c=c, rstd=rstd: e.scalar_tensor_tensor(out=hb[:, c, 0:NT], in0=xT[:, c, 0:NT], scalar=gcol(gbase + c),
                                                                   in1=rstd[:, 0:NT], op0=ALU.mult, op1=ALU.mult),
                     reads=[xc[c], rstd_c, const_c], writes=[hc[c]])

        ffn_count = [0]

        def ffn(L, which, NT):
            k = ffn_count[0]
            ffn_count[0] += 1
            gbase = (G_FFN1 if which == 0 else G_FFN2) + L * 8
            for c in range(NCH):
                if c % 2 == 0:
                    P.op("dve", lambda e, c=c: e.tensor_scalar(out=hb[:, c, 0:NT], in0=xT[:, c, 0:NT], scalar1=gcol(gbase + c), scalar2=None, op0=ALU.mult),
                         reads=[xc[c], const_c], writes=[hc[c]])
                else:
                    P.op("act", lambda e, c=c: e.activation(out=hb[:, c, 0:NT], in_=xT[:, c, 0:NT], func=AF.Identity, scale=gcol(gbase + c)),
                         reads=[xc[c], const_c], writes=[hc[c]])
            rstd_box = []
            actv = act[:, :].rearrange("p (f t) -> p f t", f=NFC)
            issue_wout(k * 8 + R_B - 1)
            pend = None

            def fin(si, fc):
                P.op("dve", lambda e, si=si, fc=fc: e.tensor_tensor(out=actv[:, fc, 0:NT], in0=usl[si][:, 0:NT], in1=sgl[si][:, 0:NT], op=ALU.mult),
                     reads=[usl_c[si], sgl_c[si]], writes=[ac[fc]])

            for fg in range(NFC // 2):
                gi = k * (NFC // 2) + fg
                issue_win(gi + R_A - 1)
                if fg % 2 == 1:
                    wm_pop(1)
                slot = gi % R_A
                for j in (0, 1):
                    fc = 2 * fg + j
                    G = PS.alloc()
                    U = PS.alloc()
                    if fc == 0:
                        for kk in range(NCH):
                            P.op("pe", [pe_mm(PS.t[G][:, 0:NT], win[slot][:, kk, 0, j * 128:(j + 1) * 128], hb[:, kk, 0:NT], kk == 0, kk == NCH - 1)],
                                 reads=[win_c[slot], hc[kk]], writes=[PS.cells[G]])
                    else:
                        P.op("pe", [pe_mm(PS.t[G][:, 0:NT], win[slot][:, kk, 0, j * 128:(j + 1) * 128], hb[:, kk, 0:NT], kk == 0, kk == NCH - 1)
                                    for kk in range(NCH)], reads=[win_c[slot], hc], writes=[PS.cells[G]])
                    P.op("pe", [pe_mm(PS.t[U][:, 0:NT], win[slot][:, kk, 1, j * 128:(j + 1) * 128], hb[:, kk, 0:NT], kk == 0, kk == NCH - 1)
                                for kk in range(NCH)], reads=[win_c[slot], hc], writes=[PS.cells[U]])
                    si = fc % 2
                    if fc == 0:
                        rstd_box.extend(norm_stats(None, None, NCH, 128, D, NT, is_x=True))
                    rstd, rstd_c = rstd_box
                    P.op("dve", lambda e, G=G, si=si, rstd=rstd: e.tensor_tensor(out=sgl[si][:, 0:NT], in0=PS.t[G][:, 0:NT], in1=rstd[:, 0:NT], op=ALU.mult),
                         reads=[PS.cells[G], rstd_c], writes=[sgl_c[si]])
                    P.op("act", lambda e, si=si: e.activation(out=sgl[si][:, 0:NT], in_=sgl[si][:, 0:NT], func=AF.Silu),
                         reads=[sgl_c[si]], writes=[sgl_c[si]])
                    P.op("dve", lambda e, U=U, si=si, rstd=rstd: e.tensor_tensor(out=usl[si][:, 0:NT], in0=PS.t[U][:, 0:NT], in1=rstd[:, 0:NT], op=ALU.mult),
                         reads=[PS.cells[U], rstd_c], writes=[usl_c[si]])
                    PS.release(G)
                    PS.release(U)
                    if pend is not None:
                        fin(*pend)
                    pend = (si, fc)
            fin(*pend)
            for dc in range(NCH):
                gi = k * 8 + dc
                issue_wout(gi + R_B - 1)
                slot = gi % R_B
                Y = PS.alloc()
                if dc == 0:
                    for (f0, f1) in ((0, 16), (16, 20), (20, NFC)):
                        P.op("pe", [pe_mm(PS.t[Y][:, 0:NT], wout[slot][:, fc, :], actv[:, fc, 0:NT], fc == 0, fc == NFC - 1)
                                    for fc in range(f0, f1)], reads=[wout_c[slot], ac[f0:f1]], writes=[PS.cells[Y]])
                else:
                    P.op("pe", [pe_mm(PS.t[Y][:, 0:NT], wout[slot][:, fc, :], actv[:, fc, 0:NT], fc == 0, fc == NFC - 1)
                                for fc in range(NFC)], reads=[wout_c[slot], ac], writes=[PS.cells[Y]])
                P.op("dve", lambda e, Y=Y, dc=dc: e.scalar_tensor_tensor(out=xT[:, dc, 0:NT], in0=PS.t[Y][:, 0:NT], scalar=0.5,
                                                                         in1=xT[:, dc, 0:NT], op0=ALU.mult, op1=ALU.add),
                     reads=[PS.cells[Y], xc[dc]], writes=[xc[dc]])
                PS.release(Y)
                x_square(dc, NT)

        xio_n = [0]

        xin = [sb(f"xin{i}", [128, D], F32) for i in range(2)]
        xin_c = [cells(1) for _ in range(2)]
        xin_sem = [newsem(f"s_xin{i}") for i in range(2)]
        xin_n = [0]
        xin_issued = {}

        def issue_x(tkey, src_rows, NT, blk):
            if (tkey, blk) in xin_issued:
                return xin_issued[(tkey, blk)]
            nb = min(128, NT - blk * 128)
            s = xin_n[0] % 2
            xin_n[0] += 1
            P.op("sp", lambda e, s=s, blk=blk, nb=nb: e.dma_start(out=xin[s][0:nb, :], in_=src_rows[blk * 128:blk * 128 + nb, :]),
                 writes=[xin_c[s]], sem=xin_sem[s], inc=16)
            xin_issued[(tkey, blk)] = s
            return s

        def load_x(tkey, src_rows, NT):
            for c_ in range(NCH):
                xsq[c_] = False
            nblk = (NT + 127) // 128
            for blk in range(nblk):
                nb = min(128, NT - blk * 128)
                s = issue_x(tkey, src_rows, NT, blk)
                for half in (0, 1):
                    b = PS.alloc()
                    P.op("pe", [(lambda e, b=b, s=s, c=c, nb=nb: e.transpose(PS.t[b][:, (c % 4) * 128:(c % 4) * 128 + nb],
                                                                             xin[s][0:nb, c * 128:(c + 1) * 128], ident[0:nb, 0:nb]))
                                for c in range(half * 4, half * 4 + 4)],
                         reads=[xin_c[s], const_c], writes=[PS.cells[b]])
                    src = PS.t[b][:, :].rearrange("p (a t) -> p a t", a=4)[:, :, 0:nb]
                    P.op("act", lambda e, src=src, half=half, blk=blk, nb=nb: e.copy(out=xT[:, half * 4:half * 4 + 4, blk * 128:blk * 128 + nb], in_=src),
                         reads=[PS.cells[b]], writes=[xc[half * 4:half * 4 + 4]])
                    PS.release(b)

        out_tickets = []

        yst = [av(i * 4096, [D], F32) for i in range(4)]
        yst_c = [acr(i * 4096, 4096) for i in range(4)]
        yst_sem = [newsem(f"s_yst{i}") for i in range(4)]

        def store_y(dst_rows, NT):
            nblk = (NT + 127) // 128
            for blk in range(nblk):
                nb = min(128, NT - blk * 128)
                for half in (0, 1):
                    b = PS.alloc()
                    P.op("pe", [(lambda e, b=b, c=c, nb=nb, blk=blk: e.transpose(PS.t[b][0:nb, (c % 4) * 128:(c % 4) * 128 + 128],
                                                                                  xT[:, c, blk * 128:blk * 128 + nb], ident[:, :]))
                                for c in range(half * 4, half * 4 + 4)],
                         reads=[xc[half * 4:half * 4 + 4], const_c], writes=[PS.cells[b]])
                    P.op("dve" if half else "act",
                         (lambda e, b=b, blk=blk, nb=nb, half=half: e.tensor_copy(out=yst[blk][0:nb, half * 512:(half + 1) * 512], in_=PS.t[b][0:nb, :]))
                         if half else
                         (lambda e, b=b, blk=blk, nb=nb, half=half: e.copy(out=yst[blk][0:nb, half * 512:(half + 1) * 512], in_=PS.t[b][0:nb, :])),
                         reads=[PS.cells[b]], writes=[yst_c[blk]])
                    PS.release(b)
                t = P.op("sp", lambda e, blk=blk, nb=nb: e.dma_start(out=dst_rows[blk * 128:blk * 128 + nb, :], in_=yst[blk][0:nb, :]),
                         reads=[yst_c[blk]], sem=yst_sem[blk], inc=16)
                out_tickets.append(t)

        def pool_mixer(L, NT, first_tile):
            wm_pop(len(wm_pending))
            rstd, rstd_c = norm_stats(None, None, NCH, 128, D, NT, is_x=True)
            W = PST + NT
            for c in range(NCH):
                w = WINDOWS[c]
                hf = pl_h[c % 2]
                hfc = pl_hc[c % 2]
                P.op("dve", lambda e, hf=hf, c=c: e.tensor_copy(out=hf[:, 0:PST], in_=st[L][:, c, :]),
                     reads=[st_c[L]], writes=[hfc])
                P.op("dve", lambda e, hf=hf, c=c, rstd=rstd: e.scalar_tensor_tensor(out=hf[:, PST:W], in0=xT[:, c, 0:NT], scalar=gcol(G_MIX + L * 8 + c),
                                                                         in1=rstd[:, 0:NT], op0=ALU.mult, op1=ALU.mult),
                     reads=[xc[c], rstd_c, const_c], writes=[hfc])
                P.op("dve", lambda e, hf=hf, c=c: e.tensor_copy(out=st[L][:, c, :], in_=hf[:, W - PST:W]),
                     reads=[hfc], writes=[st_c[L]])
                cur, curc = hf, hfc
                step = 1
                ti = 0
                while step < w:
                    nxt, nxtc = pl_t[ti % 2], pl_tc[ti % 2]
                    lo = 2 * step - 1
                    P.op("dve", lambda e, cur=cur, nxt=nxt, lo=lo, step=step: e.tensor_tensor(out=nxt[:, lo:W], in0=cur[:, lo:W], in1=cur[:, lo - step:W - step], op=ALU.add),
                         reads=[curc], writes=[nxtc])
                    cur, curc = nxt, nxtc
                    step *= 2
                    ti += 1
                P.op("dve", lambda e, cur=cur, hf=hf, c=c, w=w: e.scalar_tensor_tensor(out=hb[:, c, 0:NT], in0=cur[:, PST:W], scalar=1.0 / w,
                                                                                        in1=hf[:, PST:W], op0=ALU.mult, op1=ALU.subtract),
                     reads=[curc, hfc], writes=[hc[c]])
                if first_tile:
                    g = c // 2
                    n0 = min(16, NT)
                    P.op("dve", lambda e, cur=cur, g=g, n0=n0: e.tensor_tensor(out=pl_fix[:, 0:n0], in0=cur[:, PST:PST + n0], in1=invc[:, g * 16:g * 16 + n0], op=ALU.mult),
                         reads=[curc, const_c], writes=[pl_fixc])
                    P.op("dve", lambda e, hf=hf, c=c, n0=n0: e.tensor_tensor(out=hb[:, c, 0:n0], in0=pl_fix[:, 0:n0], in1=hf[:, PST:PST + n0], op=ALU.subtract),
                         reads=[pl_fixc, hfc], writes=[hc[c]])
            for g in range(4):
                for oc in (0, 1):
                    dc = 2 * g + oc
                    Y = PS.alloc()
                    wv = wm_view(g * 512, 2, 256)
                    P.op("pe", [pe_mm(PS.t[Y][:, 0:NT], wv[:, kk, oc * 128:(oc + 1) * 128], hb[:, 2 * g + kk, 0:NT], kk == 0, kk == 1) for kk in (0, 1)],
                         reads=[wm_c, hc[2 * g:2 * g + 2]], writes=[PS.cells[Y]])
                    P.op("dve", lambda e, Y=Y, dc=dc: e.scalar_tensor_tensor(out=xT[:, dc, 0:NT], in0=PS.t[Y][:, 0:NT], scalar=gcol(G_PSC + L * 8 + dc),
                                                                             in1=xT[:, dc, 0:NT], op0=ALU.mult, op1=ALU.add),
                         reads=[PS.cells[Y], xc[dc], const_c], writes=[xc[dc]])
                    PS.release(Y)
                    x_square(dc, NT)
            next_wm()

        pl_h = [av(i * 3072, [PST + TT], F32) for i in range(2)]
        pl_hc = [acr(i * 3072, 3072) for i in range(2)]
        pl_t = [av(6144 + i * 3072, [PST + TT], F32) for i in range(2)]
        pl_tc = [acr(6144 + i * 3072, 3072) for i in range(2)]
        pl_fix = sb("pl_fix", [128, 16], F32)
        pl_fixc = cells(1)

        def store_state(L, dst):
            s = xio_n[0] % 2
            xio_n[0] += 1
            for half in (0, 1):
                b = PS.alloc()
                P.op("pe", [(lambda e, b=b, c=c: e.transpose(PS.t[b][0:PST, (c % 4) * 128:(c % 4) * 128 + 128], st[L][:, c, :], ident[:, :]))
                            for c in range(half * 4, half * 4 + 4)],
                     reads=[st_c[L], const_c], writes=[PS.cells[b]])
                P.op("act", lambda e, b=b, s=s, half=half: e.copy(out=xio[s][0:PST, half * 512:(half + 1) * 512], in_=PS.t[b][0:PST, :]),
                     reads=[PS.cells[b]], writes=[xio_c[s]])
                PS.release(b)
            t = P.op("sp", lambda e, s=s: e.dma_start(out=dst, in_=xio[s][0:PST, :]), reads=[xio_c[s]], sem=xio_sem[s], inc=16)
            out_tickets.append(t)

        def load_state(L, src):
            s = xio_n[0] % 2
            xio_n[0] += 1
            P.op("sp", lambda e, s=s: e.dma_start(out=xio[s][0:PST, :], in_=src), writes=[xio_c[s]], sem=xio_sem[s], inc=16)
            for half in (0, 1):
                b = PS.alloc()
                P.op("pe", [(lambda e, b=b, s=s, c=c: e.transpose(PS.t[b][:, (c % 4) * 128:(c % 4) * 128 + PST], xio[s][0:PST, c * 128:(c + 1) * 128], ident[0:PST, 0:PST]))
                            for c in range(half * 4, half * 4 + 4)],
                     reads=[xio_c[s], const_c], writes=[PS.cells[b]])
                src_v = PS.t[b][:, :].rearrange("p (a t) -> p a t", a=4)[:, :, 0:PST]
                P.op("act", lambda e, src_v=src_v, half=half: e.copy(out=st[L][:, half * 4:half * 4 + 4, :], in_=src_v),
                     reads=[PS.cells[b]], writes=[st_c[L]])
                PS.release(b)

        def rope_s2a(praw, NT):
            P.op("act", lambda e: e.activation(out=sqv(1)[0:64, 0:NT], in_=PS.t[praw][0:64, 0:NT], func=AF.Square),
                 reads=[PS.cells[praw]], writes=[sqc(1)])
            b = PS.alloc()
            P.op("pe", [pe_mm(PS.t[b][0:64, 0:NT], ones[0:64, 0:64], sqv(1)[0:64, 0:NT], True, True)],
                 reads=[sqc(1), ones_c], writes=[PS.cells[b]])
            return b

        def rope_s2b(praw, b, gcolidx, NT, sl):
            P.op("act", lambda e: e.activation(out=rp_a[:, 0:NT], in_=PS.t[b][0:64, 0:NT], func=AF.Ln, bias=eps_t[0:64, 0:1], scale=1.0 / 64),
                 reads=[PS.cells[b], eps_c], writes=[rp_c[0]])
            PS.release(b)
            P.op("act", lambda e: e.activation(out=rp_b[:, 0:NT], in_=rp_a[:, 0:NT], func=AF.Exp, scale=-0.5), reads=[rp_c[0]], writes=[rp_c[1]])
            P.op("dve", lambda e: e.scalar_tensor_tensor(out=rp_n[sl][:, 0:NT], in0=PS.t[praw][0:64, 0:NT], scalar=gcol(gcolidx, 64), in1=rp_b[:, 0:NT],
                                                         op0=ALU.mult, op1=ALU.mult),
                 reads=[PS.cells[praw], rp_c[1], const_c], writes=[rp_nc[sl]])
            P.op("dve", lambda e: e.tensor_copy(out=rp_hi[sl][:, 0:NT], in_=rp_n[sl][:, 0:NT]), reads=[rp_nc[sl]], writes=[rp_hc[sl][0]])
            P.op("dve", lambda e: e.tensor_tensor(out=rp_lo[sl][:, 0:NT], in0=rp_n[sl][:, 0:NT], in1=rp_hi[sl][:, 0:NT], op=ALU.subtract),
                 reads=[rp_nc[sl], rp_hc[sl][0]], writes=[rp_hc[sl][1]])

        def rope_s3(NT, sl, out_bf, out_bf_cells, out_f32=None, out_f32_cells=None):
            r = PS.alloc()
            P.op("pe", [pe_mm(PS.t[r][0:64, 0:NT], rrotB[:, :], rp_hi[sl][:, 0:NT], True, False),
                        pe_mm(PS.t[r][0:64, 0:NT], rrotB[:, :], rp_lo[sl][:, 0:NT], False, True)],
                 reads=[rp_hc[sl], rrotB_c], writes=[PS.cells[r]])
            P.op("dve", lambda e: e.tensor_tensor(out=rp_a[:, 0:NT], in0=rp_n[sl][:, 0:NT], in1=cosS[:, 0:NT], op=ALU.mult),
                 reads=[rp_nc[sl], cs_c], writes=[rp_c[0]])
            P.op("dve", lambda e: e.tensor_tensor(out=rp_b[:, 0:NT], in0=PS.t[r][0:64, 0:NT], in1=sinS[:, 0:NT], op=ALU.mult),
                 reads=[PS.cells[r], cs_c], writes=[rp_c[1]])
            PS.release(r)
            if out_f32 is not None:
                P.op("dve", lambda e: e.tensor_tensor(out=out_f32, in0=rp_a[:, 0:NT], in1=rp_b[:, 0:NT], op=ALU.add),
                     reads=[rp_c[0], rp_c[1]], writes=[out_f32_cells])
                P.op("dve", lambda e: e.tensor_copy(out=out_bf, in_=out_f32), reads=[out_f32_cells], writes=[out_bf_cells])
            else:
                P.op("dve", lambda e: e.tensor_tensor(out=out_bf, in0=rp_a[:, 0:NT], in1=rp_b[:, 0:NT], op=ALU.add),
                     reads=[rp_c[0], rp_c[1]], writes=[out_bf_cells])

        def rope_norm(praw, gcolidx, NT, out_bf, out_bf_cells, out_f32=None, out_f32_cells=None):
            b = rope_s2a(praw, NT)
            rope_s2b(praw, b, gcolidx, NT, 0)
            rope_s3(NT, 0, out_bf, out_bf_cells, out_f32, out_f32_cells)

        cF = av(0, [2, TT], F32)
        cF_c = [acr(0, 2048), acr(2048, 2048)]
        cB = av(4096, [2, 1024], BF16)
        cB_c = acr(4096, 4096)
        krF = av(8192, [TT], F32)[0:64]
        krF_c = acr(8192, 2048)
        knn = [av(10240 + i * 1024, [TT], BF16) for i in range(2)]
        knn_c = [acr(10240 + i * 1024, 1024) for i in range(2)]
        knn_sem = [newsem(f"s_knn{i}") for i in range(2)]
        vnn = [av(12288 + i * 2048, [D], BF16) for i in range(4)]
        vnn_c = [acr(12288 + i * 2048, 2048) for i in range(4)]
        vnn_sem = [newsem(f"s_vnn{i}") for i in range(4)]
        kvn = [0, 0]

        def build_kv(coff, n, key0):
            wuk_v = wm_view(2560, 2, 1024)
            wuv_v = wm_view(4608, 2, 1024)
            def kproj(h):
                kb = PS.alloc()
                P.op("pe", [pe_mm(PS.t[kb][:, 0:n], wuk_v[:, kk, h * 128:(h + 1) * 128], cB[:, kk, coff:coff + n], kk == 0, kk == 1) for kk in (0, 1)],
                     reads=[wm_c, cB_c], writes=[PS.cells[kb]])
                return kb

            kb_next = kproj(0)
            for h in range(NH):
                kb = kb_next
                if h + 1 < NH:
                    kb_next = kproj(h + 1)
                rstd, rstd_c = norm_stats(lambda c, kb=kb: PS.t[kb][:, 0:n], [PS.cells[kb]], 1, 128, 128, n)
                s = kvn[0] % 2
                kvn[0] += 1
                P.op("dve", lambda e, kb=kb, s=s, rstd=rstd: e.scalar_tensor_tensor(out=knn[s][:, 0:n], in0=PS.t[kb][:, 0:n], scalar=gcol(G_KN), in1=rstd[:, 0:n],
                                                                                    op0=ALU.mult, op1=ALU.mult),
                     reads=[PS.cells[kb], rstd_c, const_c], writes=[knn_c[s]])
                PS.release(kb)
                P.op("sp", lambda e, s=s, h=h: e.dma_start(out=kn_s[h, :, key0:key0 + n], in_=knn[s][:, 0:n]),
                     reads=[knn_c[s]], writes=[kn_s_cells[h]], sem=knn_sem[s], inc=16)
            nblk = (n + 127) // 128
            for tb in range(nblk):
                nb = min(128, n - tb * 128)
                s = kvn[1] % 4
                kvn[1] += 1
                for half in (0, 1):
                    vb = PS.alloc()
                    P.op("pe", [pe_mm(PS.t[vb][0:nb, :], cB[:, kk, coff + tb * 128:coff + tb * 128 + nb], wuv_v[:, kk, half * 512:(half + 1) * 512], kk == 0, kk == 1)
                                for kk in (0, 1)], reads=[wm_c, cB_c], writes=[PS.cells[vb]])
                    P.op("act", lambda e, vb=vb, s=s, nb=nb, half=half: e.copy(out=vnn[s][0:nb, half * 512:(half + 1) * 512], in_=PS.t[vb][0:nb, :]),
                         reads=[PS.cells[vb]], writes=[vnn_c[s]])
                    PS.release(vb)
                chunk = key0 // 128 + tb
                dst = v_s[:, 0:nb, chunk, :].rearrange("h p d -> p h d")
                srcv = vnn[s][0:nb, :].rearrange("p (h d) -> p h d", h=NH)
                P.op("sp", lambda e, dst=dst, srcv=srcv: e.dma_start(out=dst, in_=srcv),
                     reads=[vnn_c[s]], writes=[v_s_cells[chunk]], sem=vnn_sem[s], inc=16)

        osm_n = [0]

        def latent(NT, pos0, ckv_dst, kr_dst):
            wm_pop(len(wm_pending))
            x_norm_to_hb(G_KV, NT)
            wdkv_v = wm_view(0, 8, 256)
            wkr_v = wm_view(2048, 8, 64)
            cb = []
            for oc in (0, 1):
                b = PS.alloc()
                cb.append(b)
                P.op("pe", [pe_mm(PS.t[b][:, 0:NT], wdkv_v[:, kk, oc * 128:(oc + 1) * 128], hb[:, kk, 0:NT], kk == 0, kk == NCH - 1) for kk in range(NCH)],
                     reads=[wm_c, hc], writes=[PS.cells[b]])
            rstd, rstd_c = norm_stats(lambda c: PS.t[cb[c]][:, 0:NT], [PS.cells[cb[0]], PS.cells[cb[1]]], 2, 128, KVL, NT)
            for oc in (0, 1):
                P.op("dve", lambda e, oc=oc, rstd=rstd: e.scalar_tensor_tensor(out=cF[:, oc, 0:NT], in0=PS.t[cb[oc]][:, 0:NT], scalar=gcol(G_C + oc), in1=rstd[:, 0:NT],
                                                                    op0=ALU.mult, op1=ALU.mult),
                     reads=[PS.cells[cb[oc]], rstd_c, const_c], writes=[cF_c[oc]])
                PS.release(cb[oc])
            P.op("act", lambda e: e.copy(out=cB[:, :, 0:NT], in_=cF[:, :, 0:NT]), reads=[cF_c], writes=[cB_c])
            b = PS.alloc()
            P.op("pe", [pe_mm(PS.t[b][0:64, 0:NT], wkr_v[:, kk, :], hb[:, kk, 0:NT], kk == 0, kk == NCH - 1) for kk in range(NCH)],
                 reads=[wm_c, hc], writes=[PS.cells[b]])
            rope_norm(b, G_KR, NT, Kr[:, pos0:pos0 + NT], kr_c, out_f32=krF[:, 0:NT], out_f32_cells=krF_c)
            PS.release(b)
            nblk = (NT + 127) // 128
            for blk in range(nblk):
                nb = min(128, NT - blk * 128)
                s = osm_n[0] % 2
                osm_n[0] += 1
                b = PS.alloc()
                fns = [(lambda e, b=b, oc=oc, blk=blk, nb=nb: e.transpose(PS.t[b][0:nb, oc * 128:(oc + 1) * 128], cF[:, oc, blk * 128:blk * 128 + nb], ident[:, :]))
                       for oc in (0, 1)]
                fns.append(lambda e, b=b, blk=blk, nb=nb: e.transpose(PS.t[b][0:nb, 256:320], krF[:, blk * 128:blk * 128 + nb], ident[0:64, 0:64]))
                P.op("pe", fns, reads=[cF_c, krF_c, const_c], writes=[PS.cells[b]])
                P.op("act", lambda e, b=b, s=s, nb=nb: e.copy(out=osm[s][0:nb, :], in_=PS.t[b][0:nb, 0:320]),
                     reads=[PS.cells[b]], writes=[osm_c[s]])
                PS.release(b)
                t = P.op("sp", [lambda e, s=s, blk=blk, nb=nb: e.dma_start(out=ckv_dst[blk * 128:blk * 128 + nb, :], in_=osm[s][0:nb, 0:256]),
                                lambda e, s=s, blk=blk, nb=nb: e.dma_start(out=kr_dst[blk * 128:blk * 128 + nb, :], in_=osm[s][0:nb, 256:320])],
                         reads=[osm_c[s]], sem=osm_sem[s], inc=16)
                out_tickets.append(t)
            build_kv(0, NT, pos0)
            next_wm()

        def sample_past():
            wm_pop(len(wm_pending))
            cp32 = av(8192, [8, 256], F32)
            kp32 = av(16384, [8, 64], F32)
            spc = acr(8192, 10240)
            P.op("sp", [lambda e: e.dma_start(out=cp32, in_=cckv.rearrange("(a p) c -> p a c", p=128)),
                        lambda e: e.dma_start(out=kp32, in_=ckr.rearrange("(a p) c -> p a c", p=128))],
                 writes=[spc], sem=misc_sem, inc=16)
            for a in range(8):
                b = PS.alloc()
                P.op("pe", [(lambda e, b=b, a=a, oc=oc: e.transpose(PS.t[b][:, oc * 128:(oc + 1) * 128], cp32[:, a, oc * 128:(oc + 1) * 128], ident[:, :]))
                            for oc in (0, 1)], reads=[spc, const_c], writes=[PS.cells[b]])
                srcv = PS.t[b][:, 0:256].rearrange("p (o t) -> p o t", o=2)
                P.op("act", lambda e, srcv=srcv, a=a: e.copy(out=cB[:, :, a * 128:(a + 1) * 128], in_=srcv),
                     reads=[PS.cells[b]], writes=[cB_c])
                PS.release(b)
            for half in (0, 1):
                b = PS.alloc()
                P.op("pe", [(lambda e, b=b, a=a: e.transpose(PS.t[b][0:64, (a % 4) * 128:(a % 4) * 128 + 128], kp32[:, a, :], ident[:, :]))
                            for a in range(half * 4, half * 4 + 4)], reads=[spc, const_c], writes=[PS.cells[b]])
                P.op("act", lambda e, b=b, half=half: e.copy(out=Kr[:, half * 512:(half + 1) * 512], in_=PS.t[b][0:64, :]),
                     reads=[PS.cells[b]], writes=[kr_c])
                PS.release(b)
            build_kv(0, 512, 0)
            build_kv(512, 512, 512)

        qn = av(0, [NH, TT], BF16)
        qn_c = [ac[h] for h in range(NH)]
        qr = av(8192, [NH, TT], BF16)[0:64]
        qr_c = [ac[8 + h] for h in range(NH)]
        qlat = av(16384, [3, TT], BF16)
        qlat_c = acr(16384, 3072)
        rden = sb("rden", [128, TT], F32)
        rden_c = cells(1)
        pT_n = [0]

        def issue_kv_load(h, nk):
            s = h % 2
            nkb = (nk + 127) // 128
            P.op("sp", [lambda e: e.dma_start(out=kvk[s][:, 0:nk], in_=kn_s[h, :, 0:nk]),
                        lambda e: e.dma_start(out=kvv[s][:, 0:nkb, :], in_=v_s[h, :, 0:nkb, :])],
                 reads=[kn_s_cells[h], v_s_cells[0:nkb]], writes=[kv_c[s]], sem=kv_sem[s], inc=16)

        def attention(i, L, NT, pos0):
            wm_pop(len(wm_pending))
            nk = pos0 + NT
            nkb = (nk + 127) // 128
            x_norm_to_hb(G_MIX + L * 8, NT)
            wdq_v = wm_view(0, 8, 384)
            wuq_v = wm_view(3072, 3, 1536)
            wo_v = wm_view(7680, 8, 1024)
            issue_kv_load(0, nk)
            qb = []
            for oc in range(3):
                b = PS.alloc()
                qb.append(b)
                P.op("pe", [pe_mm(PS.t[b][:, 0:NT], wdq_v[:, kk, oc * 128:(oc + 1) * 128], hb[:, kk, 0:NT], kk == 0, kk == NCH - 1) for kk in range(NCH)],
                     reads=[wm_c, hc], writes=[PS.cells[b]])
            rstd, rstd_c = norm_stats(lambda c: PS.t[qb[c]][:, 0:NT], [PS.cells[q] for q in qb], 3, 128, QL, NT)
            for oc in range(3):
                P.op("dve", lambda e, oc=oc, rstd=rstd: e.scalar_tensor_tensor(out=qlat[:, oc, 0:NT], in0=PS.t[qb[oc]][:, 0:NT], scalar=gcol(G_QL + i * 3 + oc),
                                                                    in1=rstd[:, 0:NT], op0=ALU.mult, op1=ALU.mult),
                     reads=[PS.cells[qb[oc]], rstd_c, const_c], writes=[qlat_c])
                PS.release(qb[oc])
            def qproj(h):
                bn = PS.alloc()
                P.op("pe", [pe_mm(PS.t[bn][:, 0:NT], wuq_v[:, kk, h * 192:h * 192 + 128], qlat[:, kk, 0:NT], kk == 0, kk == 2) for kk in range(3)],
                     reads=[wm_c, qlat_c], writes=[PS.cells[bn]])
                br = PS.alloc()
                P.op("pe", [pe_mm(PS.t[br][0:64, 0:NT], wuq_v[:, kk, h * 192 + 128:h * 192 + 192], qlat[:, kk, 0:NT], kk == 0, kk == 2) for kk in range(3)],
                     reads=[wm_c, qlat_c], writes=[PS.cells[br]])
                return bn, br

            def q_s2(h, bn, br):
                b_ = rope_s2a(br, NT)
                rstd, rstd_c = norm_stats(lambda c, bn=bn: PS.t[bn][:, 0:NT], [PS.cells[bn]], 1, 128, 128, NT)
                P.op("dve", lambda e, bn=bn, h=h, rstd=rstd: e.scalar_tensor_tensor(out=qn[:, h, 0:NT], in0=PS.t[bn][:, 0:NT], scalar=gcol(G_QN + i), in1=rstd[:, 0:NT],
                                                                                    op0=ALU.mult, op1=ALU.mult),
                     reads=[PS.cells[bn], rstd_c, const_c], writes=[qn_c[h]])
                PS.release(bn)
                rope_s2b(br, b_, G_QR + i, NT, h % 2)
                PS.release(br)

            pj = {0: qproj(0), 1: qproj(1)}
            q_s2(0, *pj[0])
            for h in range(NH):
                if h + 2 < NH:
                    pj[h + 2] = qproj(h + 2)
                if h + 1 < NH:
                    q_s2(h + 1, *pj[h + 1])
                rope_s3(NT, h % 2, qr[:, h, 0:NT], qr_c[h])
            oT = hb
            for h in range(NH if 'core' not in SKIP else 0):
                s = h % 2
                if h + 1 < NH:
                    issue_kv_load(h + 1, nk)
                O = PS.alloc()
                Dn = PS.alloc()
                blocks = []
                for kb in range(nkb):
                    k0 = kb * 128
                    kn = min(128, nk - k0)
                    c0 = 0 if k0 < pos0 else (k0 - pos0)
                    if c0 >= NT:
                        continue
                    blocks.append((kb, k0, kn, c0))
                pend = None
                first = True
                for bi, (kb, k0, kn, c0) in enumerate(blocks):
                    S = PS.alloc()
                    ncol = NT - c0
                    P.op("pe", [pe_mm(PS.t[S][0:kn, 0:ncol], kvk[s][:, k0:k0 + kn], qn[:, h, c0:NT], True, False),
                                pe_mm(PS.t[S][0:kn, 0:ncol], Kr[:, k0:k0 + kn], qr[:, h, c0:NT], False, True)],
                         reads=[kv_c[s], kr_c, qn_c[h], qr_c[h]], writes=[PS.cells[S]])
                    ps_ = pT_n[0] % 4
                    pT_n[0] += 1
                    P.op("act", lambda e, S=S, ps_=ps_, kn=kn, ncol=ncol: e.activation(out=pT[ps_][0:kn, 0:ncol], in_=PS.t[S][0:kn, 0:ncol], func=AF.Exp, scale=ATTN_SCALE),
                         reads=[PS.cells[S]], writes=[pT_c[ps_]])
                    PS.release(S)
                    if k0 >= pos0 and kn > 64:
                        P.op("dve", lambda e, ps_=ps_, kn=kn: e.memset(pT[ps_][64:kn, 0:64], 0.0), writes=[pT_c[ps_]])
                    cur = (ps_, kb, kn, c0, ncol)
                    if pend is not None:
                        emit_ov(pend, O, Dn, s, first, False)
                        first = False
                    pend = cur
                emit_ov(pend, O, Dn, s, first, True)
                P.op("dve", lambda e, Dn=Dn: e.reciprocal(out=rden[:, 0:NT], in_=PS.t[Dn][:, 0:NT]), reads=[PS.cells[Dn]], writes=[rden_c])
                P.op("dve", lambda e, O=O, h=h: e.tensor_tensor(out=oT[:, h, 0:NT], in0=PS.t[O][:, 0:NT], in1=rden[:, 0:NT], op=ALU.mult),
                     reads=[PS.cells[O], rden_c], writes=[hc[h]])
                PS.release(O)
                PS.release(Dn)
            for dc in range(NCH):
                Y = PS.alloc()
                P.op("pe", [pe_mm(PS.t[Y][:, 0:NT], wo_v[:, h, dc * 128:(dc + 1) * 128], oT[:, h, 0:NT], h == 0, h == NH - 1) for h in range(NH)],
                     reads=[wm_c, hc], writes=[PS.cells[Y]])
                P.op("dve", lambda e, Y=Y, dc=dc: e.tensor_tensor(out=xT[:, dc, 0:NT], in0=PS.t[Y][:, 0:NT], in1=xT[:, dc, 0:NT], op=ALU.add),
                     reads=[PS.cells[Y], xc[dc]], writes=[xc[dc]])
                PS.release(Y)
                x_square(dc, NT)
            next_wm()

        def emit_ov(pend, O, Dn, s, first, last):
            ps_, kb, kn, c0, ncol = pend
            NTl = c0 + ncol
            P.op("pe", [pe_mm(PS.t[O][:, c0:NTl], kvv[s][0:kn, kb, :], pT[ps_][0:kn, 0:ncol], first, last),
                        pe_mm(PS.t[Dn][:, c0:NTl], ones[0:kn, :], pT[ps_][0:kn, 0:ncol], first, last)],
                 reads=[kv_c[s], pT_c[ps_], ones_c], writes=[PS.cells[O], PS.cells[Dn]])

        next_wm(now=True)
        def tparams(ti_):
            kind, sidx, t = tiles[ti_]
            if kind == "p":
                NT = TT
                pos0 = t * TT
                row0 = sidx * SEQ + pos0
                return dict(kind=kind, sidx=sidx, t=t, NT=NT, pos0=pos0, xsrc=xp[row0:row0 + NT, :], ydst=yp[row0:row0 + NT, :],
                            cdst=ckvp[row0:row0 + NT, :], kdst=krp[row0:row0 + NT, :], first_tile=(t == 0), last_tile=(t == TPS - 1))
            return dict(kind=kind, sidx=sidx, t=t, NT=SNT, pos0=PAST, xsrc=xs, ydst=ys, cdst=ckvs, kdst=krs, first_tile=False, last_tile=True)

        for ti in range(len(tiles)):
            tp_ = tparams(ti)
            kind, sidx, t, NT, pos0 = tp_["kind"], tp_["sidx"], tp_["t"], tp_["NT"], tp_["pos0"]
            xsrc, ydst, cdst, kdst = tp_["xsrc"], tp_["ydst"], tp_["cdst"], tp_["kdst"]
            first_tile, last_tile = tp_["first_tile"], tp_["last_tile"]
            load_x(ti, xsrc, NT)
            P.op("sp", [lambda e, pos0=pos0, NT=NT: e.dma_start(out=cosS[:, 0:NT], in_=cos_d[:, pos0:pos0 + NT]),
                        lambda e, pos0=pos0, NT=NT: e.dma_start(out=sinS[:, 0:NT], in_=sin_d[:, pos0:pos0 + NT])],
                 writes=[cs_c], sem=cs_sem, inc=16)
            if kind == "p" and t == 0:
                for L in range(N_A):
                    P.op("dve", lambda e, L=L: e.memset(st[L][:], 0.0), writes=[st_c[L]])
            if kind == "s":
                for L in range(N_A):
                    load_state(L, spool[L])
            for L in range(DEPTH):
                if L >= MAXL:
                    break
                if L == N_A and "latent" not in SKIP:
                    if kind == "s":
                        sample_past()
                    latent(NT, pos0, cdst, kdst)
                if "ffn" not in SKIP:
                    ffn(L, 0, NT)
                if L < N_A:
                    if "pool" not in SKIP:
                        pool_mixer(L, NT, first_tile)
                        if last_tile:
                            store_state(L, npp[L, sidx] if kind == "p" else nps[L])
                elif "attn" not in SKIP:
                    attention(L - N_A, L, NT, pos0)
                if "ffn2" not in SKIP and "ffn" not in SKIP:
                    if L == DEPTH - 1 and ti + 1 < len(tiles):
                        nx = tparams(ti + 1)
                        for blk_ in range(min(2, (nx["NT"] + 127) // 128)):
                            issue_x(ti + 1, nx["xsrc"], nx["NT"], blk_)
                    ffn(L, 1, NT)
            store_y(ydst, NT)

        for tk in out_tickets:
            P.wait_ticket("sp", tk)

        @block.tensor
        def _(e):
            P.replay("pe", e)

        @block.scalar
        def _(e):
            P.replay("act", e)

        @block.vector
        def _(e):
            P.replay("dve", e)

        @block.gpsimd
        def _(e):
            P.replay("pool", e)

        @block.sync
        def _(e):
            P.replay("sp", e)
    return nc


def _consts():
    ident = np.eye(128, dtype=np.float32)
    rrot = np.zeros((64, 64), np.float32)
    for m in range(32):
        rrot[m + 32, m] = -1.0
        rrot[m, m + 32] = 1.0
    inv = (10000.0 ** (-(np.arange(0, 64, 2, dtype=np.float32) / np.float32(64)))).astype(np.float32)
    pos = np.arange(2048, dtype=np.float32)
    ang = (pos[None, :] * inv[:, None]).astype(np.float32)
    cosT = np.concatenate([np.cos(ang), np.cos(ang)], 0).astype(np.float32)
    sinT = np.concatenate([np.sin(ang), np.sin(ang)], 0).astype(np.float32)
    invc = np.zeros((128, 64), np.float32)
    for g, w in enumerate((2, 4, 8, 16)):
        invc[:, g * 16:(g + 1) * 16] = 1.0 / np.minimum(np.arange(16) + 1, w).astype(np.float32)
    return ident, rrot, cosT, sinT, invc


def _gains(inp):
    g = np.zeros((128, NG), np.float32)

    def put(col, vec):
        v = np.asarray(vec, np.float32).reshape(-1)
        if v.size >= 128:
            n = v.size // 128
            g[:, col:col + n] = v.reshape(n, 128).T
        else:
            g[0:v.size, col] = v
    for L in range(DEPTH):
        put(G_FFN1 + L * 8, inp["ffn1_norm"][L])
        put(G_MIX + L * 8, inp["mix_norm"][L])
        put(G_FFN2 + L * 8, inp["ffn2_norm"][L])
    for L in range(N_A):
        put(G_PSC + L * 8, inp["pool_scale"][L])
    put(G_KV, inp["kv_norm"])
    put(G_C, inp["c_norm"])
    for i in range(2):
        put(G_QL + i * 3, inp["q_lat_norm"][i])
        put(G_QN + i, inp["qn_norm"][i])
        put(G_QR + i, inp["qr_norm"][i])
    put(G_KN, inp["kn_norm"])
    put(G_KR, inp["kr_norm"])
    return g


def run(inp, n_cores, NSEQ, SEQ, trace=False, with_sample=True):
    nc = build(NSEQ, SEQ, with_sample=with_sample)
    ident, rrot, cosT, sinT, invc = _consts()
    gains = _gains(inp)
    f = lambda a: np.ascontiguousarray(np.asarray(a, np.float32))
    shared = {
        "w1in": f(inp["ffn1_w_in"]), "w1out": f(inp["ffn1_w_out"]), "w2in": f(inp["ffn2_w_in"]), "w2out": f(inp["ffn2_w_out"]),
        "poolw": f(inp["pool_w"]), "wdkv": f(inp["w_dkv"]), "wkr": f(inp["w_kr"]), "wuk": f(inp["w_uk"]), "wuv": f(inp["w_uv"]),
        "wdq": f(inp["w_dq"]), "wuq": f(inp["w_uq"]), "wo": f(inp["w_o"]),
        "gains": gains, "ident": ident, "rrot": rrot, "cosT": cosT, "sinT": sinT, "invcnt": invc,
    }
    xpr = f(inp["x_prompt"])
    in_maps = []
    for c in range(n_cores):
        m = dict(shared)
        m["xp"] = np.ascontiguousarray(xpr[c * NSEQ:(c + 1) * NSEQ].reshape(NSEQ * SEQ, D))
        m["xs"] = f(inp["x_sample"][c])
        m["spool"] = f(inp["state_pool"][:, c])
        m["cckv"] = f(inp["cache_ckv"][c])
        m["ckr"] = f(inp["cache_krope"][c])
        in_maps.append(m)
    res = run_bass_kernel_spmd(nc, in_maps, core_ids=list(range(n_cores)), trace=trace)
    R = res.results
    y_prompt = np.concatenate([r["yp"].reshape(NSEQ, SEQ, D) for r in R], 0)
    y_sample = np.stack([r["ys"] for r in R], 0)
    npp = np.concatenate([r["npp"] for r in R], 1)
    nps = np.stack([r["nps"] for r in R], 1)
    ckvp = np.concatenate([r["ckvp"].reshape(NSEQ, SEQ, KVL) for r in R], 0)
    krp = np.concatenate([r["krp"].reshape(NSEQ, SEQ, ROPE) for r in R], 0)
    ckvs = np.stack([r["ckvs"] for r in R], 0)
    krs = np.stack([r["krs"] for r in R], 0)
    outs = (y_prompt, y_sample, npp, nps, ckvp, krp, ckvs, krs)
    return tuple(np.ascontiguousarray(o.astype(np.float32)) for o in outs), res


def kernel(**inputs):
    outs, _ = run(inputs, N_CORES, 4, 2048)
    return outs
```

```python
import contextlib
import numpy as np
import concourse.bass as bass
import concourse.mybir as mybir
from concourse.bass_utils import run_bass_kernel_spmd

F32 = mybir.dt.float32
BF16 = mybir.dt.bfloat16
AF = mybir.ActivationFunctionType
ALU = mybir.AluOpType

D = 1024
NCH = 8
DFF = 2816
NFC = 22
NH = 8
KVL = 256
QL = 384
ROPE = 64
DEPTH = 4
N_A = 2
PST = 15
EPS = 1e-6
ATTN_SCALE = float((128 + 64) ** -0.5)
WINDOWS = (2, 2, 4, 4, 8, 8, 16, 16)
TT = 512
N_CORES = 8

G_FFN1, G_MIX, G_FFN2, G_PSC, G_KV, G_C, G_QL, G_KN, G_QN, G_KR, G_QR, NG = 0, 32, 64, 96, 112, 120, 122, 128, 129, 131, 132, 134


class Sem:
    def __init__(self, h):
        self.h = h
        self.n = 0


class Cell:
    __slots__ = ("w", "r")

    def __init__(self):
        self.w = None
        self.r = []


def cells(n):
    return [Cell() for _ in range(n)]


class Eng:
    def __init__(self, name, sem, is_pe=False):
        self.name = name
        self.sem = sem
        self.ops = []
        self.waited = {}
        self.is_pe = is_pe


class Prog:
    def __init__(self):
        self.eng = {}

    def _flat(self, xs):
        out = []
        for x in xs:
            if isinstance(x, Cell):
                out.append(x)
            else:
                out.extend(self._flat(x))
        return out

    def _deps(self, e, reads, writes):
        deps = {}

        def add(t, war):
            if t is None:
                return
            s, v = t
            if s is e.sem:
                if e.is_pe:
                    return
            if deps.get(s, 0) < v:
                deps[s] = v

        for c in reads:
            add(c.w, False)
        for c in writes:
            add(c.w, False)
            for t in c.r:
                add(t, True)
        for s, v in deps.items():
            if e.waited.get(s, 0) < v:
                e.waited[s] = v
                e.ops.append(("w", s.h, v))

    def op(self, ename, fns, reads=(), writes=(), sem=None, inc=1):
        e = self.eng[ename]
        reads = self._flat(reads)
        writes = self._flat(writes)
        self._deps(e, reads, writes)
        if not isinstance(fns, (list, tuple)):
            fns = [fns]
        s = sem if sem is not None else e.sem
        if sem is not None:
            for f in fns:
                s.n += inc
                e.ops.append(("i", f, s.h, inc))
        else:
            for f in fns[:-1]:
                e.ops.append(("i", f, None, 0))
            s.n += inc
            e.ops.append(("i", fns[-1], s.h, inc))
        t = (s, s.n)
        for c in reads:
            c.r.append(t)
        for c in writes:
            c.w = t
            c.r = []
        return t

    def wait_ticket(self, ename, t):
        e = self.eng[ename]
        s, v = t
        if e.waited.get(s, 0) < v:
            e.waited[s] = v
            e.ops.append(("w", s.h, v))

    def replay(self, ename, hw):
        for o in self.eng[ename].ops:
            if o[0] == "w":
                hw.wait_ge(o[1], o[2])
            else:
                ins = o[1](hw)
                if o[2] is not None:
                    ins.then_inc(o[2], o[3])


class PsumPool:
    def __init__(self, tensors):
        self.t = tensors
        self.cells = [Cell() for _ in tensors]
        self.free = list(range(len(tensors)))

    def alloc(self):
        i = self.free.pop(0)
        return i

    def release(self, i):
        self.free.append(i)


def build(NSEQ, SEQ, with_sample=True, PAST=1024, SNT=32):
    import os
    SKIP = set(os.environ.get('KSKIP', '').split(','))
    MAXL = int(os.environ.get('KMAXL', '4'))
    NROPE = int(os.environ.get('KNROPE', '8'))
    nc = bass.Bass("TRN2", target_bir_lowering=False)
    NPT = NSEQ * SEQ
    TPS = SEQ // TT

    def din(name, shape, dt=F32):
        return nc.dram_tensor(name, list(shape), dt, kind="ExternalInput").ap()

    def dout(name, shape, dt=F32):
        return nc.dram_tensor(name, list(shape), dt, kind="ExternalOutput").ap()

    xp = din("xp", [NPT, D])
    xs = din("xs", [SNT, D])
    spool = din("spool", [N_A, PST, D])
    cckv = din("cckv", [PAST, KVL])
    ckr = din("ckr", [PAST, ROPE])
    w1in = din("w1in", [DEPTH, D, 2 * DFF])
    w1out = din("w1out", [DEPTH, DFF, D])
    w2in = din("w2in", [DEPTH, D, 2 * DFF])
    w2out = din("w2out", [DEPTH, DFF, D])
    poolw = din("poolw", [N_A, 4, 256, 256])
    wdkv = din("wdkv", [D, KVL])
    wkr = din("wkr", [D, ROPE])
    wuk = din("wuk", [KVL, D])
    wuv = din("wuv", [KVL, D])
    wdq = din("wdq", [2, D, QL])
    wuq = din("wuq", [2, QL, NH * 192])
    wo = din("wo", [2, D, D])
    gains_d = din("gains", [128, NG])
    ident_d = din("ident", [128, 128])
    rrot_d = din("rrot", [64, 64])
    cos_d = din("cosT", [64, 2048])
    sin_d = din("sinT", [64, 2048])
    invc_d = din("invcnt", [128, 64])

    yp = dout("yp", [NPT, D])
    ys = dout("ys", [SNT, D])
    npp = dout("npp", [N_A, NSEQ, PST, D])
    nps = dout("nps", [N_A, PST, D])
    ckvp = dout("ckvp", [NPT, KVL])
    krp = dout("krp", [NPT, ROPE])
    ckvs = dout("ckvs", [SNT, KVL])
    krs = dout("krs", [SNT, ROPE])

    kn_s = nc.dram_tensor("kn_s", [NH, 128, 2048], BF16, kind="Internal").ap()
    v_s = nc.dram_tensor("v_s", [NH, 128, 16, 128], BF16, kind="Internal").ap()
    kn_s_cells = cells(NH)
    v_s_cells = cells(16)

    P = Prog()
    with contextlib.ExitStack() as es:
        def sb(name, shape, dt):
            return es.enter_context(nc.sbuf_tensor("sb_" + name, list(shape), dt))

        def newsem(name):
            return Sem(es.enter_context(nc.semaphore(name)))

        P.eng["pe"] = Eng("pe", newsem("s_pe"), is_pe=True)
        P.eng["act"] = Eng("act", newsem("s_act"))
        P.eng["dve"] = Eng("dve", newsem("s_dve"))
        P.eng["pool"] = Eng("pool", newsem("s_pool"))
        P.eng["sp"] = Eng("sp", newsem("s_sp"))

        xT = sb("xT", [128, NCH, TT], F32)
        xc = cells(NCH)
        hb = sb("hb", [128, NCH, TT], BF16)
        hc = cells(NCH)
        act = sb("act", [128, NFC * TT], BF16)
        ac = cells(NFC)
        R_A, R_B = 3, 4
        win = [sb(f"win{i}", [128, NCH, 2, 256], BF16) for i in range(R_A)]
        win_c = [cells(1) for _ in range(R_A)]
        win_sem = [newsem(f"s_win{i}") for i in range(R_A)]
        wout = [sb(f"wout{i}", [128, NFC, 128], BF16) for i in range(R_B)]
        wout_c = [cells(1) for _ in range(R_B)]
        wout_sem = [newsem(f"s_wout{i}") for i in range(R_B)]
        WM = sb("WM", [128, 15872], BF16)
        wm_c = cells(4)
        wm_sems = [newsem(f"s_wm{i}") for i in range(4)]
        kvk = [sb(f"kvk{i}", [128, 2048], BF16) for i in range(2)]
        kvv = [sb(f"kvv{i}", [128, 16, 128], BF16) for i in range(2)]
        kv_c = [cells(1) for _ in range(2)]
        kv_sem = [newsem(f"s_kv{i}") for i in range(2)]
        Kr = sb("Kr", [64, 2048], BF16)
        kr_c = cells(1)
        pT = [sb(f"pT{i}", [128, TT], BF16) for i in range(4)]
        pT_c = [cells(1) for _ in range(4)]
        xio = [sb(f"xio{i}", [128, D], F32) for i in range(1)]
        xio_c = [cells(1) for _ in range(1)]
        xio_sem = [newsem(f"s_xio{i}") for i in range(1)]
        rstd_r = [sb(f"rstd{i}", [128, TT], F32) for i in range(2)]
        rstd_rc = [cells(1) for _ in range(2)]
        nrm_n = [0]
        sgl = [sb(f"sgl{i}", [128, TT], F32) for i in range(2)]
        sgl_c = [cells(1) for _ in range(2)]
        usl = [sb(f"usl{i}", [128, TT], F32) for i in range(2)]
        usl_c = [cells(1) for _ in range(2)]
        sqx = sb("sqx", [128, NCH, TT], BF16)
        sqx_c = cells(NCH)
        xsq = [False] * NCH
        ident = sb("ident", [128, 128], F32)
        ones = sb("ones", [128, 128], BF16)
        rrot = sb("rrot", [64, 64], F32)
        gains = sb("gains", [128, NG], F32)
        invc = sb("invc", [128, 64], F32)
        const_c = cells(1)
        const_sem = newsem("s_const")
        cosS = sb("cosS", [64, TT], F32)
        sinS = sb("sinS", [64, TT], F32)
        cs_c = cells(1)
        cs_sem = newsem("s_cs")
        st = [sb(f"st{l}", [128, NCH, PST], F32) for l in range(N_A)]
        st_c = [cells(NCH) for _ in range(N_A)]
        osm = [sb(f"osm{i}", [128, 320], F32) for i in range(2)]
        osm_c = [cells(1) for _ in range(2)]
        osm_sem = [newsem(f"s_osm{i}") for i in range(2)]
        rp_a = sb("rp_a", [64, TT], F32)
        rp_b = sb("rp_b", [64, TT], F32)
        rp_c = cells(2)
        rp_n = [sb(f"rp_n{i}", [64, TT], F32) for i in range(2)]
        rp_nc = [cells(1) for _ in range(2)]
        rp_hi = [sb(f"rp_hi{i}", [64, TT], BF16) for i in range(2)]
        rp_lo = [sb(f"rp_lo{i}", [64, TT], BF16) for i in range(2)]
        rp_hc = [cells(2) for _ in range(2)]
        rrotB = sb("rrotB", [64, 64], BF16)
        rrotB_c = cells(1)
        misc_sem = newsem("s_misc")
        scr_sem = newsem("s_scr")

        def av(byte_off, free_shape, dt):
            n = int(np.prod(free_shape))
            if dt == BF16:
                a_ = act[:, byte_off // 2:byte_off // 2 + n]
            else:
                a_ = act[:, byte_off // 2:byte_off // 2 + 2 * n].bitcast(F32)
            if len(free_shape) == 2:
                a_ = a_.rearrange("p (a b) -> p a b", a=free_shape[0])
            return a_

        def acr(byte_off, nbytes):
            return ac[byte_off // 1024:(byte_off + nbytes + 1023) // 1024]

        def sqv(c):
            return av((21 - c) * 1024, [TT], BF16)

        def sqc(c):
            return ac[21 - c]

        PS = PsumPool([es.enter_context(nc.psum_tensor(f"ps{i}", [128, TT], F32)) for i in range(8)])

        block = es.enter_context(nc.Block())

        def pe_mm(out, lhsT, rhs, start, stop):
            return lambda e: e.matmul(out, lhsT, rhs, start=start, stop=stop)

        def gcol(col, np_=128):
            return gains[0:np_, col:col + 1]

        P.op("sp", [lambda e: e.dma_start(out=ident[:], in_=ident_d),
                    lambda e: e.dma_start(out=rrot[:], in_=rrot_d),
                    lambda e: e.dma_start(out=gains[:], in_=gains_d),
                    lambda e: e.dma_start(out=invc[:], in_=invc_d)],
             writes=[const_c], sem=const_sem, inc=16)
        ones_c = cells(1)
        P.op("dve", lambda e: e.memset(ones[:], 1.0), writes=[ones_c])
        P.op("dve", lambda e: e.tensor_copy(out=rrotB[:], in_=rrot[:]), reads=[const_c], writes=[rrotB_c])

        win_list = []
        wout_list = []
        tiles = []
        for s in range(NSEQ):
            for t in range(TPS):
                tiles.append(("p", s, t))
        if with_sample:
            tiles.append(("s", 0, 0))
        for _ in tiles:
            for L in range(DEPTH):
                for which in (0, 1):
                    for fg in range(NFC // 2):
                        win_list.append((L, which, fg))
                    for dg in range(8):
                        wout_list.append((L, which, dg))
        win_issued = [0]
        wout_issued = [0]

        def issue_win(upto):
            while win_issued[0] <= upto and win_issued[0] < len(win_list):
                i = win_issued[0]
                L, which, fg = win_list[i]
                w = (w1in if which == 0 else w2in)[L].rearrange("(k p) f -> p k f", p=128)
                slot = i % R_A
                fns = []
                for g in (0, 1):
                    c0 = g * DFF + fg * 256
                    fns.append(lambda e, slot=slot, g=g, c0=c0, w=w: e.dma_start(out=win[slot][:, :, g, :], in_=w[:, :, c0:c0 + 256]))
                P.op("pool", fns, writes=[win_c[slot]], sem=win_sem[slot], inc=16)
                win_issued[0] += 1

        def issue_wout(upto):
            while wout_issued[0] <= upto and wout_issued[0] < len(wout_list):
                i = wout_issued[0]
                L, which, dg = wout_list[i]
                w = (w1out if which == 0 else w2out)[L].rearrange("(f p) d -> p f d", p=128)
                slot = i % R_B
                P.op("pool", lambda e, slot=slot, dg=dg, w=w: e.dma_start(out=wout[slot][:], in_=w[:, :, dg * 128:(dg + 1) * 128]),
                     writes=[wout_c[slot]], sem=wout_sem[slot], inc=16)
                wout_issued[0] += 1

        def wm_view(off, k, f):
            return WM[:, off:off + k * f].rearrange("p (k f) -> p k f", k=k)

        def load_wm(kind, idx):
            fns = []
            if kind == "pool":
                for g in range(4):
                    src_ = poolw[idx, g].rearrange("(k p) d -> p k d", p=128)
                    fns.append(lambda e, g=g, src_=src_: e.dma_start(out=wm_view(g * 512, 2, 256), in_=src_))
            elif kind == "lat":
                fns.append(lambda e: e.dma_start(out=wm_view(0, 8, 256), in_=wdkv.rearrange("(k p) f -> p k f", p=128)))
                fns.append(lambda e: e.dma_start(out=wm_view(2048, 8, 64), in_=wkr.rearrange("(k p) f -> p k f", p=128)))
                fns.append(lambda e: e.dma_start(out=wm_view(2560, 2, 1024), in_=wuk.rearrange("(k p) f -> p k f", p=128)))
                fns.append(lambda e: e.dma_start(out=wm_view(4608, 2, 1024), in_=wuv.rearrange("(k p) f -> p k f", p=128)))
            else:
                fns.append(lambda e: e.dma_start(out=wm_view(0, 8, 384), in_=wdq[idx].rearrange("(k p) f -> p k f", p=128)))
                fns.append(lambda e: e.dma_start(out=wm_view(3072, 3, 1536), in_=wuq[idx].rearrange("(k p) f -> p k f", p=128)))
                wo_r = wo[idx].rearrange("(k p) f -> p k f", p=128)
                fns.append(lambda e: e.dma_start(out=wm_view(7680, 8, 1024)[:, 0:4, :], in_=wo_r[:, 0:4, :]))
                fns.append(lambda e: e.dma_start(out=wm_view(7680, 8, 1024)[:, 4:8, :], in_=wo_r[:, 4:8, :]))
            return fns

        wm_pending = []
        wm_stride = [2]
        wm_piece = [0]

        def wm_pop(n=1):
            for _ in range(n):
                if wm_pending:
                    pc = wm_piece[0] % 4
                    wm_piece[0] += 1
                    P.op("pool", [wm_pending.pop(0)], writes=[wm_c[pc]], sem=wm_sems[pc], inc=16)

        wm_users = []
        for _ in tiles:
            wm_users += [("pool", 0), ("pool", 1), ("lat", 0), ("attn", 0), ("attn", 1)]
        wm_ptr = [0]

        def next_wm(now=False):
            if wm_ptr[0] < len(wm_users):
                u = wm_users[wm_ptr[0]]
                wm_pending.extend(load_wm(*u))
                wm_stride[0] = 4 if u == ("attn", 1) else 2
                wm_ptr[0] += 1
            if now:
                wm_pop(len(wm_pending))

        def x_square(c, NT):
            P.op("act", lambda e, c=c: e.activation(out=sqx[:, c, 0:NT], in_=xT[:, c, 0:NT], func=AF.Square),
                 reads=[xc[c]], writes=[sqx_c[c]])
            xsq[c] = True

        def norm_stats(src_fn, src_cells, nchunks, npart, dim, NT, is_x=False):
            if is_x:
                for c in range(nchunks):
                    if not xsq[c]:
                        x_square(c, NT)
                sq_ap = lambda c: sqx[:, c, 0:NT]
                sq_cl = lambda c: sqx_c[c]
            else:
                for c in range(nchunks):
                    src_ap = src_fn(c)
                    P.op("act", lambda e, c=c, src_ap=src_ap: e.activation(out=sqv(c)[0:npart, 0:NT], in_=src_ap, func=AF.Square),
                         reads=[src_cells[c]], writes=[sqc(c)])
                sq_ap = lambda c: sqv(c)[0:npart, 0:NT]
                sq_cl = sqc
            b = PS.alloc()
            P.op("pe", [pe_mm(PS.t[b][0:npart, 0:NT], ones[0:npart, 0:npart], sq_ap(c), c == 0, c == nchunks - 1)
                        for c in range(nchunks)],
                 reads=[[sq_cl(c) for c in range(nchunks)], ones_c], writes=[PS.cells[b]])
            ri = nrm_n[0] % 2
            nrm_n[0] += 1
            rstd, rstd_c = rstd_r[ri], rstd_rc[ri]
            P.op("act", lambda e: e.activation(out=rstd[0:npart, 0:NT], in_=PS.t[b][0:npart, 0:NT], func=AF.Ln,
                                               bias=eps_t[0:npart, 0:1], scale=1.0 / dim),
                 reads=[PS.cells[b], eps_c], writes=[rstd_c])
            PS.release(b)
            P.op("act", lambda e: e.activation(out=rstd[0:npart, 0:NT], in_=rstd[0:npart, 0:NT], func=AF.Exp, scale=-0.5),
                 reads=[rstd_c], writes=[rstd_c])
            return rstd, rstd_c

        eps_t = sb("eps_t", [128, 1], F32)
        eps_c = cells(1)
        P.op("dve", lambda e: e.memset(eps_t[:], EPS), writes=[eps_c])

        def x_norm_to_hb(gbase, NT):
            rstd, rstd_c = norm_stats(None, None, NCH, 128, D, NT, is_x=True)
            for c in range(NCH):
                P.op("dve", lambda e, c=c, rstd=rstd: e.scalar_tensor_tensor(out=hb[:, c, 0:NT], in0=xT[:, c, 0:NT], scalar=gcol(gbase + c),
                                                                   in1=rstd[:, 0:NT], op0=ALU.mult, op1=ALU.mult),
                     reads=[xc[c], rstd_c, const_c], writes=[hc[c]])

        ffn_count = [0]

        def ffn(L, which, NT):
            k = ffn_count[0]
            ffn_count[0] += 1
            gbase = (G_FFN1 if which == 0 else G_FFN2) + L * 8
            for c in range(NCH):
                if c % 2 == 0:
                    P.op("dve", lambda e, c=c: e.tensor_scalar(out=hb[:, c, 0:NT], in0=xT[:, c, 0:NT], scalar1=gcol(gbase + c), scalar2=None, op0=ALU.mult),
                         reads=[xc[c], const_c], writes=[hc[c]])
                else:
                    P.op("act", lambda e, c=c: e.activation(out=hb[:, c, 0:NT], in_=xT[:, c, 0:NT], func=AF.Identity, scale=gcol(gbase + c)),
                         reads=[xc[c], const_c], writes=[hc[c]])
            rstd_box = []
            actv = act[:, :].rearrange("p (f t) -> p f t", f=NFC)
            issue_wout(k * 8 + R_B - 1)
            pend = None

            def fin(si, fc):
                P.op("dve", lambda e, si=si, fc=fc: e.tensor_tensor(out=actv[:, fc, 0:NT], in0=usl[si][:, 0:NT], in1=sgl[si][:, 0:NT], op=ALU.mult),
                     reads=[usl_c[si], sgl_c[si]], writes=[ac[fc]])

            for fg in range(NFC // 2):
                gi = k * (NFC // 2) + fg
                issue_win(gi + R_A - 1)
                if fg % wm_stride[0] == 1:
                    wm_pop(1)
                slot = gi % R_A
                for j in (0, 1):
                    fc = 2 * fg + j
                    G = PS.alloc()
                    U = PS.alloc()
                    if fc == 0:
                        for kk in range(NCH):
                            P.op("pe", [pe_mm(PS.t[G][:, 0:NT], win[slot][:, kk, 0, j * 128:(j + 1) * 128], hb[:, kk, 0:NT], kk == 0, kk == NCH - 1)],
                                 reads=[win_c[slot], hc[kk]], writes=[PS.cells[G]])
                    else:
                        P.op("pe", [pe_mm(PS.t[G][:, 0:NT], win[slot][:, kk, 0, j * 128:(j + 1) * 128], hb[:, kk, 0:NT], kk == 0, kk == NCH - 1)
                                    for kk in range(NCH)], reads=[win_c[slot], hc], writes=[PS.cells[G]])
                    P.op("pe", [pe_mm(PS.t[U][:, 0:NT], win[slot][:, kk, 1, j * 128:(j + 1) * 128], hb[:, kk, 0:NT], kk == 0, kk == NCH - 1)
                                for kk in range(NCH)], reads=[win_c[slot], hc], writes=[PS.cells[U]])
                    si = fc % 2
                    if fc == 0:
                        rstd_box.extend(norm_stats(None, None, NCH, 128, D, NT, is_x=True))
                    rstd, rstd_c = rstd_box
                    P.op("dve", lambda e, G=G, si=si, rstd=rstd: e.tensor_tensor(out=sgl[si][:, 0:NT], in0=PS.t[G][:, 0:NT], in1=rstd[:, 0:NT], op=ALU.mult),
                         reads=[PS.cells[G], rstd_c], writes=[sgl_c[si]])
                    P.op("act", lambda e, si=si: e.activation(out=sgl[si][:, 0:NT], in_=sgl[si][:, 0:NT], func=AF.Silu),
                         reads=[sgl_c[si]], writes=[sgl_c[si]])
                    P.op("dve", lambda e, U=U, si=si, rstd=rstd: e.tensor_tensor(out=usl[si][:, 0:NT], in0=PS.t[U][:, 0:NT], in1=rstd[:, 0:NT], op=ALU.mult),
                         reads=[PS.cells[U], rstd_c], writes=[usl_c[si]])
                    PS.release(G)
                    PS.release(U)
                    if pend is not None:
                        fin(*pend)
                    pend = (si, fc)
            fin(*pend)
            for dc in range(NCH):
                gi = k * 8 + dc
                issue_wout(gi + R_B - 1)
                slot = gi % R_B
                Y = PS.alloc()
                if dc == 0:
                    for (f0, f1) in ((0, 16), (16, 20), (20, NFC)):
                        P.op("pe", [pe_mm(PS.t[Y][:, 0:NT], wout[slot][:, fc, :], actv[:, fc, 0:NT], fc == 0, fc == NFC - 1)
                                    for fc in range(f0, f1)], reads=[wout_c[slot], ac[f0:f1]], writes=[PS.cells[Y]])
                else:
                    P.op("pe", [pe_mm(PS.t[Y][:, 0:NT], wout[slot][:, fc, :], actv[:, fc, 0:NT], fc == 0, fc == NFC - 1)
                                for fc in range(NFC)], reads=[wout_c[slot], ac], writes=[PS.cells[Y]])
                P.op("dve", lambda e, Y=Y, dc=dc: e.scalar_tensor_tensor(out=xT[:, dc, 0:NT], in0=PS.t[Y][:, 0:NT], scalar=0.5,
                                                                         in1=xT[:, dc, 0:NT], op0=ALU.mult, op1=ALU.add),
                     reads=[PS.cells[Y], xc[dc]], writes=[xc[dc]])
                PS.release(Y)
                x_square(dc, NT)

        xio_n = [0]

        xin = [sb(f"xin{i}", [128, D], F32) for i in range(2)]
        xin_c = [cells(1) for _ in range(2)]
        xin_sem = [newsem(f"s_xin{i}") for i in range(2)]
        xin_n = [0]
        xin_issued = {}

        def issue_x(tkey, src_rows, NT, blk):
            if (tkey, blk) in xin_issued:
                return xin_issued[(tkey, blk)]
            nb = min(128, NT - blk * 128)
            s = xin_n[0] % 2
            xin_n[0] += 1
            P.op("sp", lambda e, s=s, blk=blk, nb=nb: e.dma_start(out=xin[s][0:nb, :], in_=src_rows[blk * 128:blk * 128 + nb, :]),
                 writes=[xin_c[s]], sem=xin_sem[s], inc=16)
            xin_issued[(tkey, blk)] = s
            return s

        def load_x(tkey, src_rows, NT):
            for c_ in range(NCH):
                xsq[c_] = False
            nblk = (NT + 127) // 128
            for blk in range(nblk):
                nb = min(128, NT - blk * 128)
                s = issue_x(tkey, src_rows, NT, blk)
                for half in (0, 1):
                    b = PS.alloc()
                    P.op("pe", [(lambda e, b=b, s=s, c=c, nb=nb: e.transpose(PS.t[b][:, (c % 4) * 128:(c % 4) * 128 + nb],
                                                                             xin[s][0:nb, c * 128:(c + 1) * 128], ident[0:nb, 0:nb]))
                                for c in range(half * 4, half * 4 + 4)],
                         reads=[xin_c[s], const_c], writes=[PS.cells[b]])
                    src = PS.t[b][:, :].rearrange("p (a t) -> p a t", a=4)[:, :, 0:nb]
                    P.op("act", lambda e, src=src, half=half, blk=blk, nb=nb: e.copy(out=xT[:, half * 4:half * 4 + 4, blk * 128:blk * 128 + nb], in_=src),
                         reads=[PS.cells[b]], writes=[xc[half * 4:half * 4 + 4]])
                    PS.release(b)

        out_tickets = []

        yst = [av(i * 4096, [D], F32) for i in range(4)]
        yst_c = [acr(i * 4096, 4096) for i in range(4)]
        yst_sem = [newsem(f"s_yst{i}") for i in range(4)]

        def store_y(dst_rows, NT):
            nblk = (NT + 127) // 128
            for blk in range(nblk):
                nb = min(128, NT - blk * 128)
                for half in (0, 1):
                    b = PS.alloc()
                    P.op("pe", [(lambda e, b=b, c=c, nb=nb, blk=blk: e.transpose(PS.t[b][0:nb, (c % 4) * 128:(c % 4) * 128 + 128],
                                                                                  xT[:, c, blk * 128:blk * 128 + nb], ident[:, :]))
                                for c in range(half * 4, half * 4 + 4)],
                         reads=[xc[half * 4:half * 4 + 4], const_c], writes=[PS.cells[b]])
                    P.op("dve" if half else "act",
                         (lambda e, b=b, blk=blk, nb=nb, half=half: e.tensor_copy(out=yst[blk][0:nb, half * 512:(half + 1) * 512], in_=PS.t[b][0:nb, :]))
                         if half else
                         (lambda e, b=b, blk=blk, nb=nb, half=half: e.copy(out=yst[blk][0:nb, half * 512:(half + 1) * 512], in_=PS.t[b][0:nb, :])),
                         reads=[PS.cells[b]], writes=[yst_c[blk]])
                    PS.release(b)
                t = P.op("sp", lambda e, blk=blk, nb=nb: e.dma_start(out=dst_rows[blk * 128:blk * 128 + nb, :], in_=yst[blk][0:nb, :]),
                         reads=[yst_c[blk]], sem=yst_sem[blk], inc=16)
                out_tickets.append(t)

        def pool_mixer(L, NT, first_tile):
            wm_pop(len(wm_pending))
            rstd, rstd_c = norm_stats(None, None, NCH, 128, D, NT, is_x=True)
            W = PST + NT
            def chunk_front(c):
                on_pool = c >= 6
                if on_pool:
                    hf, hfc = pl_hP[c - 6], pl_hPc[c - 6]
                else:
                    hf, hfc = pl_h[c % 2], pl_hc[c % 2]
                P.op("act", lambda e, hf=hf, c=c: e.copy(out=hf[:, 0:PST], in_=st[L][:, c, :]),
                     reads=[st_c[L][c]], writes=[hfc])
                P.op("dve", lambda e, hf=hf, c=c, rstd=rstd: e.scalar_tensor_tensor(out=hf[:, PST:W], in0=xT[:, c, 0:NT], scalar=gcol(G_MIX + L * 8 + c),
                                                                                 in1=rstd[:, 0:NT], op0=ALU.mult, op1=ALU.mult),
                     reads=[xc[c], rstd_c, const_c], writes=[hfc])
                P.op("act", lambda e, hf=hf, c=c: e.copy(out=st[L][:, c, :], in_=hf[:, W - PST:W]),
                     reads=[hfc], writes=[st_c[L][c]])
                return hf, hfc

            def chunk_adds(c, hf, hfc):
                w = WINDOWS[c]
                on_pool = c >= 6
                eng = "pool" if on_pool else "dve"
                tbuf, tbufc = (pl_tP[c - 6], pl_tPc[c - 6]) if on_pool else (pl_t, pl_tc)
                cur, curc = hf, hfc
                step = 1
                ti = 0
                while step < w:
                    nxt, nxtc = tbuf[ti % 2], tbufc[ti % 2]
                    lo = 2 * step - 1
                    P.op(eng, lambda e, cur=cur, nxt=nxt, lo=lo, step=step: e.tensor_tensor(out=nxt[:, lo:W], in0=cur[:, lo:W], in1=cur[:, lo - step:W - step], op=ALU.add),
                         reads=[curc], writes=[nxtc])
                    cur, curc = nxt, nxtc
                    step *= 2
                    ti += 1
                return cur, curc

            def chunk_back(c, hf, hfc, cur, curc):
                w = WINDOWS[c]
                P.op("dve", lambda e, cur=cur, hf=hf, c=c, w=w: e.scalar_tensor_tensor(out=hb[:, c, 0:NT], in0=cur[:, PST:W], scalar=1.0 / w,
                                                                                        in1=hf[:, PST:W], op0=ALU.mult, op1=ALU.subtract),
                     reads=[curc, hfc], writes=[hc[c]])
                if first_tile:
                    g = c // 2
                    n0 = min(16, NT)
                    P.op("dve", lambda e, cur=cur, g=g, n0=n0: e.tensor_tensor(out=pl_fix[:, 0:n0], in0=cur[:, PST:PST + n0], in1=invc[:, g * 16:g * 16 + n0], op=ALU.mult),
                         reads=[curc, const_c], writes=[pl_fixc])
                    P.op("dve", lambda e, hf=hf, c=c, n0=n0: e.tensor_tensor(out=hb[:, c, 0:n0], in0=pl_fix[:, 0:n0], in1=hf[:, PST:PST + n0], op=ALU.subtract),
                         reads=[pl_fixc, hfc], writes=[hc[c]])

            late = []
            for c in (0, 1, 6, 7, 2, 3, 4, 5):
                hf, hfc = chunk_front(c)
                cur, curc = chunk_adds(c, hf, hfc)
                if c >= 6:
                    late.append((c, hf, hfc, cur, curc))
                else:
                    chunk_back(c, hf, hfc, cur, curc)
            for item in late:
                chunk_back(*item)
            for g in range(4):
                for oc in (0, 1):
                    dc = 2 * g + oc
                    Y = PS.alloc()
                    wv = wm_view(g * 512, 2, 256)
                    P.op("pe", [pe_mm(PS.t[Y][:, 0:NT], wv[:, kk, oc * 128:(oc + 1) * 128], hb[:, 2 * g + kk, 0:NT], kk == 0, kk == 1) for kk in (0, 1)],
                         reads=[wm_c, hc[2 * g:2 * g + 2]], writes=[PS.cells[Y]])
                    P.op("dve", lambda e, Y=Y, dc=dc: e.scalar_tensor_tensor(out=xT[:, dc, 0:NT], in0=PS.t[Y][:, 0:NT], scalar=gcol(G_PSC + L * 8 + dc),
                                                                             in1=xT[:, dc, 0:NT], op0=ALU.mult, op1=ALU.add),
                         reads=[PS.cells[Y], xc[dc], const_c], writes=[xc[dc]])
                    PS.release(Y)
                    x_square(dc, NT)
            next_wm()

        pl_h = [av(i * 3072, [PST + TT], F32) for i in range(2)]
        pl_hc = [acr(i * 3072, 3072) for i in range(2)]
        pl_t = [av(6144 + i * 3072, [PST + TT], F32) for i in range(2)]
        pl_tc = [acr(6144 + i * 3072, 3072) for i in range(2)]
        pl_fix = sb("pl_fix", [128, 16], F32)
        pl_fixc = cells(1)
        pl_hP = [av(12288, [PST + TT], F32), av(15360, [PST + TT], F32)]
        pl_hPc = [acr(12288, 3072), acr(15360, 3072)]
        pl_tPa = [av(18432, [PST + TT], F32), sb("pl_tP1", [128, PST + TT], F32)]
        pl_tPb = [sb("pl_tP2", [128, PST + TT], F32), sb("pl_tP3", [128, PST + TT], F32)]
        pl_tP = [pl_tPa, pl_tPb]
        pl_tPc = [[acr(18432, 3072), cells(1)], [cells(1), cells(1)]]

        def store_state(L, dst):
            s = 0
            xio_n[0] += 1
            for half in (0, 1):
                b = PS.alloc()
                P.op("pe", [(lambda e, b=b, c=c: e.transpose(PS.t[b][0:PST, (c % 4) * 128:(c % 4) * 128 + 128], st[L][:, c, :], ident[:, :]))
                            for c in range(half * 4, half * 4 + 4)],
                     reads=[st_c[L], const_c], writes=[PS.cells[b]])
                P.op("act", lambda e, b=b, s=s, half=half: e.copy(out=xio[s][0:PST, half * 512:(half + 1) * 512], in_=PS.t[b][0:PST, :]),
                     reads=[PS.cells[b]], writes=[xio_c[s]])
                PS.release(b)
            t = P.op("sp", lambda e, s=s: e.dma_start(out=dst, in_=xio[s][0:PST, :]), reads=[xio_c[s]], sem=xio_sem[s], inc=16)
            out_tickets.append(t)

        def load_state(L, src):
            s = 0
            xio_n[0] += 1
            P.op("sp", lambda e, s=s: e.dma_start(out=xio[s][0:PST, :], in_=src), writes=[xio_c[s]], sem=xio_sem[s], inc=16)
            for half in (0, 1):
                b = PS.alloc()
                P.op("pe", [(lambda e, b=b, s=s, c=c: e.transpose(PS.t[b][:, (c % 4) * 128:(c % 4) * 128 + PST], xio[s][0:PST, c * 128:(c + 1) * 128], ident[0:PST, 0:PST]))
                            for c in range(half * 4, half * 4 + 4)],
                     reads=[xio_c[s], const_c], writes=[PS.cells[b]])
                src_v = PS.t[b][:, :].rearrange("p (a t) -> p a t", a=4)[:, :, 0:PST]
                P.op("act", lambda e, src_v=src_v, half=half: e.copy(out=st[L][:, half * 4:half * 4 + 4, :], in_=src_v),
                     reads=[PS.cells[b]], writes=[st_c[L]])
                PS.release(b)

        def rope_s2a(praw, NT):
            P.op("act", lambda e: e.activation(out=sqv(1)[0:64, 0:NT], in_=PS.t[praw][0:64, 0:NT], func=AF.Square),
                 reads=[PS.cells[praw]], writes=[sqc(1)])
            b = PS.alloc()
            P.op("pe", [pe_mm(PS.t[b][0:64, 0:NT], ones[0:64, 0:64], sqv(1)[0:64, 0:NT], True, True)],
                 reads=[sqc(1), ones_c], writes=[PS.cells[b]])
            return b

        def rope_s2b(praw, b, gcolidx, NT, sl):
            P.op("act", lambda e: e.activation(out=rp_a[:, 0:NT], in_=PS.t[b][0:64, 0:NT], func=AF.Ln, bias=eps_t[0:64, 0:1], scale=1.0 / 64),
                 reads=[PS.cells[b], eps_c], writes=[rp_c[0]])
            PS.release(b)
            P.op("act", lambda e: e.activation(out=rp_b[:, 0:NT], in_=rp_a[:, 0:NT], func=AF.Exp, scale=-0.5), reads=[rp_c[0]], writes=[rp_c[1]])
            P.op("dve", lambda e: e.scalar_tensor_tensor(out=rp_n[sl][:, 0:NT], in0=PS.t[praw][0:64, 0:NT], scalar=gcol(gcolidx, 64), in1=rp_b[:, 0:NT],
                                                         op0=ALU.mult, op1=ALU.mult),
                 reads=[PS.cells[praw], rp_c[1], const_c], writes=[rp_nc[sl]])
            P.op("dve", lambda e: e.tensor_copy(out=rp_hi[sl][:, 0:NT], in_=rp_n[sl][:, 0:NT]), reads=[rp_nc[sl]], writes=[rp_hc[sl][0]])
            P.op("dve", lambda e: e.tensor_tensor(out=rp_lo[sl][:, 0:NT], in0=rp_n[sl][:, 0:NT], in1=rp_hi[sl][:, 0:NT], op=ALU.subtract),
                 reads=[rp_nc[sl], rp_hc[sl][0]], writes=[rp_hc[sl][1]])

        def rope_s3(NT, sl, out_bf, out_bf_cells, out_f32=None, out_f32_cells=None):
            r = PS.alloc()
            P.op("pe", [pe_mm(PS.t[r][0:64, 0:NT], rrotB[:, :], rp_hi[sl][:, 0:NT], True, False),
                        pe_mm(PS.t[r][0:64, 0:NT], rrotB[:, :], rp_lo[sl][:, 0:NT], False, True)],
                 reads=[rp_hc[sl], rrotB_c], writes=[PS.cells[r]])
            P.op("dve", lambda e: e.tensor_tensor(out=rp_a[:, 0:NT], in0=rp_n[sl][:, 0:NT], in1=cosS[:, 0:NT], op=ALU.mult),
                 reads=[rp_nc[sl], cs_c], writes=[rp_c[0]])
            P.op("dve", lambda e: e.tensor_tensor(out=rp_b[:, 0:NT], in0=PS.t[r][0:64, 0:NT], in1=sinS[:, 0:NT], op=ALU.mult),
                 reads=[PS.cells[r], cs_c], writes=[rp_c[1]])
            PS.release(r)
            if out_f32 is not None:
                P.op("dve", lambda e: e.tensor_tensor(out=out_f32, in0=rp_a[:, 0:NT], in1=rp_b[:, 0:NT], op=ALU.add),
                     reads=[rp_c[0], rp_c[1]], writes=[out_f32_cells])
                P.op("dve", lambda e: e.tensor_copy(out=out_bf, in_=out_f32), reads=[out_f32_cells], writes=[out_bf_cells])
            else:
                P.op("dve", lambda e: e.tensor_tensor(out=out_bf, in0=rp_a[:, 0:NT], in1=rp_b[:, 0:NT], op=ALU.add),
                     reads=[rp_c[0], rp_c[1]], writes=[out_bf_cells])

        def rope_norm(praw, gcolidx, NT, out_bf, out_bf_cells, out_f32=None, out_f32_cells=None):
            b = rope_s2a(praw, NT)
            rope_s2b(praw, b, gcolidx, NT, 0)
            rope_s3(NT, 0, out_bf, out_bf_cells, out_f32, out_f32_cells)

        cF = av(0, [2, TT], F32)
        cF_c = [acr(0, 2048), acr(2048, 2048)]
        cB = av(4096, [2, 1024], BF16)
        cB_c = acr(4096, 4096)
        krF = av(8192, [TT], F32)[0:64]
        krF_c = acr(8192, 2048)
        knn = [av(10240 + i * 1024, [TT], BF16) for i in range(2)]
        knn_c = [acr(10240 + i * 1024, 1024) for i in range(2)]
        knn_sem = [newsem(f"s_knn{i}") for i in range(2)]
        vnn = [av(12288 + i * 2048, [D], BF16) for i in range(4)]
        vnn_c = [acr(12288 + i * 2048, 2048) for i in range(4)]
        vnn_sem = [newsem(f"s_vnn{i}") for i in range(4)]
        kvn = [0, 0]

        def build_kv(coff, n, key0):
            wuk_v = wm_view(2560, 2, 1024)
            wuv_v = wm_view(4608, 2, 1024)
            def kproj(h):
                kb = PS.alloc()
                P.op("pe", [pe_mm(PS.t[kb][:, 0:n], wuk_v[:, kk, h * 128:(h + 1) * 128], cB[:, kk, coff:coff + n], kk == 0, kk == 1) for kk in (0, 1)],
                     reads=[wm_c, cB_c], writes=[PS.cells[kb]])
                return kb

            kb_next = kproj(0)
            for h in range(NH):
                kb = kb_next
                if h + 1 < NH:
                    kb_next = kproj(h + 1)
                rstd, rstd_c = norm_stats(lambda c, kb=kb: PS.t[kb][:, 0:n], [PS.cells[kb]], 1, 128, 128, n)
                s = kvn[0] % 2
                kvn[0] += 1
                P.op("dve", lambda e, kb=kb, s=s, rstd=rstd: e.scalar_tensor_tensor(out=knn[s][:, 0:n], in0=PS.t[kb][:, 0:n], scalar=gcol(G_KN), in1=rstd[:, 0:n],
                                                                                    op0=ALU.mult, op1=ALU.mult),
                     reads=[PS.cells[kb], rstd_c, const_c], writes=[knn_c[s]])
                PS.release(kb)
                P.op("sp", lambda e, s=s, h=h: e.dma_start(out=kn_s[h, :, key0:key0 + n], in_=knn[s][:, 0:n]),
                     reads=[knn_c[s]], writes=[kn_s_cells[h]], sem=knn_sem[s], inc=16)
            nblk = (n + 127) // 128
            for tb in range(nblk):
                nb = min(128, n - tb * 128)
                s = kvn[1] % 4
                kvn[1] += 1
                for half in (0, 1):
                    vb = PS.alloc()
                    P.op("pe", [pe_mm(PS.t[vb][0:nb, :], cB[:, kk, coff + tb * 128:coff + tb * 128 + nb], wuv_v[:, kk, half * 512:(half + 1) * 512], kk == 0, kk == 1)
                                for kk in (0, 1)], reads=[wm_c, cB_c], writes=[PS.cells[vb]])
                    P.op("act", lambda e, vb=vb, s=s, nb=nb, half=half: e.copy(out=vnn[s][0:nb, half * 512:(half + 1) * 512], in_=PS.t[vb][0:nb, :]),
                         reads=[PS.cells[vb]], writes=[vnn_c[s]])
                    PS.release(vb)
                chunk = key0 // 128 + tb
                dst = v_s[:, 0:nb, chunk, :].rearrange("h p d -> p h d")
                srcv = vnn[s][0:nb, :].rearrange("p (h d) -> p h d", h=NH)
                P.op("sp", lambda e, dst=dst, srcv=srcv: e.dma_start(out=dst, in_=srcv),
                     reads=[vnn_c[s]], writes=[v_s_cells[chunk]], sem=vnn_sem[s], inc=16)

        osm_n = [0]

        def latent(NT, pos0, ckv_dst, kr_dst):
            wm_pop(len(wm_pending))
            x_norm_to_hb(G_KV, NT)
            wdkv_v = wm_view(0, 8, 256)
            wkr_v = wm_view(2048, 8, 64)
            cb = []
            for oc in (0, 1):
                b = PS.alloc()
                cb.append(b)
                P.op("pe", [pe_mm(PS.t[b][:, 0:NT], wdkv_v[:, kk, oc * 128:(oc + 1) * 128], hb[:, kk, 0:NT], kk == 0, kk == NCH - 1) for kk in range(NCH)],
                     reads=[wm_c, hc], writes=[PS.cells[b]])
            rstd, rstd_c = norm_stats(lambda c: PS.t[cb[c]][:, 0:NT], [PS.cells[cb[0]], PS.cells[cb[1]]], 2, 128, KVL, NT)
            for oc in (0, 1):
                P.op("dve", lambda e, oc=oc, rstd=rstd: e.scalar_tensor_tensor(out=cF[:, oc, 0:NT], in0=PS.t[cb[oc]][:, 0:NT], scalar=gcol(G_C + oc), in1=rstd[:, 0:NT],
                                                                    op0=ALU.mult, op1=ALU.mult),
                     reads=[PS.cells[cb[oc]], rstd_c, const_c], writes=[cF_c[oc]])
                PS.release(cb[oc])
            P.op("act", lambda e: e.copy(out=cB[:, :, 0:NT], in_=cF[:, :, 0:NT]), reads=[cF_c], writes=[cB_c])
            b = PS.alloc()
            P.op("pe", [pe_mm(PS.t[b][0:64, 0:NT], wkr_v[:, kk, :], hb[:, kk, 0:NT], kk == 0, kk == NCH - 1) for kk in range(NCH)],
                 reads=[wm_c, hc], writes=[PS.cells[b]])
            rope_norm(b, G_KR, NT, Kr[:, pos0:pos0 + NT], kr_c, out_f32=krF[:, 0:NT], out_f32_cells=krF_c)
            PS.release(b)
            nblk = (NT + 127) // 128
            for blk in range(nblk):
                nb = min(128, NT - blk * 128)
                s = osm_n[0] % 2
                osm_n[0] += 1
                b = PS.alloc()
                fns = [(lambda e, b=b, oc=oc, blk=blk, nb=nb: e.transpose(PS.t[b][0:nb, oc * 128:(oc + 1) * 128], cF[:, oc, blk * 128:blk * 128 + nb], ident[:, :]))
                       for oc in (0, 1)]
                fns.append(lambda e, b=b, blk=blk, nb=nb: e.transpose(PS.t[b][0:nb, 256:320], krF[:, blk * 128:blk * 128 + nb], ident[0:64, 0:64]))
                P.op("pe", fns, reads=[cF_c, krF_c, const_c], writes=[PS.cells[b]])
                P.op("act", lambda e, b=b, s=s, nb=nb: e.copy(out=osm[s][0:nb, :], in_=PS.t[b][0:nb, 0:320]),
                     reads=[PS.cells[b]], writes=[osm_c[s]])
                PS.release(b)
                t = P.op("sp", [lambda e, s=s, blk=blk, nb=nb: e.dma_start(out=ckv_dst[blk * 128:blk * 128 + nb, :], in_=osm[s][0:nb, 0:256]),
                                lambda e, s=s, blk=blk, nb=nb: e.dma_start(out=kr_dst[blk * 128:blk * 128 + nb, :], in_=osm[s][0:nb, 256:320])],
                         reads=[osm_c[s]], sem=osm_sem[s], inc=16)
                out_tickets.append(t)
            build_kv(0, NT, pos0)
            next_wm()

        def sample_past():
            wm_pop(len(wm_pending))
            cp32 = av(8192, [8, 256], F32)
            kp32 = av(16384, [8, 64], F32)
            spc = acr(8192, 10240)
            P.op("sp", [lambda e: e.dma_start(out=cp32, in_=cckv.rearrange("(a p) c -> p a c", p=128)),
                        lambda e: e.dma_start(out=kp32, in_=ckr.rearrange("(a p) c -> p a c", p=128))],
                 writes=[spc], sem=misc_sem, inc=16)
            for a in range(8):
                b = PS.alloc()
                P.op("pe", [(lambda e, b=b, a=a, oc=oc: e.transpose(PS.t[b][:, oc * 128:(oc + 1) * 128], cp32[:, a, oc * 128:(oc + 1) * 128], ident[:, :]))
                            for oc in (0, 1)], reads=[spc, const_c], writes=[PS.cells[b]])
                srcv = PS.t[b][:, 0:256].rearrange("p (o t) -> p o t", o=2)
                P.op("act", lambda e, srcv=srcv, a=a: e.copy(out=cB[:, :, a * 128:(a + 1) * 128], in_=srcv),
                     reads=[PS.cells[b]], writes=[cB_c])
                PS.release(b)
            for half in (0, 1):
                b = PS.alloc()
                P.op("pe", [(lambda e, b=b, a=a: e.transpose(PS.t[b][0:64, (a % 4) * 128:(a % 4) * 128 + 128], kp32[:, a, :], ident[:, :]))
                            for a in range(half * 4, half * 4 + 4)], reads=[spc, const_c], writes=[PS.cells[b]])
                P.op("act", lambda e, b=b, half=half: e.copy(out=Kr[:, half * 512:(half + 1) * 512], in_=PS.t[b][0:64, :]),
                     reads=[PS.cells[b]], writes=[kr_c])
                PS.release(b)
            build_kv(0, 512, 0)
            build_kv(512, 512, 512)

        qn = av(0, [NH, TT], BF16)
        qn_c = [ac[h] for h in range(NH)]
        qr = av(8192, [NH, TT], BF16)[0:64]
        qr_c = [ac[8 + h] for h in range(NH)]
        qlat = av(16384, [3, TT], BF16)
        qlat_c = acr(16384, 3072)
        rden = rstd_r[0]
        rden_c = rstd_rc[0]
        pT_n = [0]

        def issue_kv_load(h, nk):
            s = h % 2
            nkb = (nk + 127) // 128
            P.op("sp", [lambda e: e.dma_start(out=kvk[s][:, 0:nk], in_=kn_s[h, :, 0:nk]),
                        lambda e: e.dma_start(out=kvv[s][:, 0:nkb, :], in_=v_s[h, :, 0:nkb, :])],
                 reads=[kn_s_cells[h], v_s_cells[0:nkb]], writes=[kv_c[s]], sem=kv_sem[s], inc=16)

        def attention(i, L, NT, pos0):
            wm_pop(len(wm_pending))
            nk = pos0 + NT
            nkb = (nk + 127) // 128
            x_norm_to_hb(G_MIX + L * 8, NT)
            wdq_v = wm_view(0, 8, 384)
            wuq_v = wm_view(3072, 3, 1536)
            wo_v = wm_view(7680, 8, 1024)
            issue_kv_load(0, nk)
            qb = []
            for oc in range(3):
                b = PS.alloc()
                qb.append(b)
                P.op("pe", [pe_mm(PS.t[b][:, 0:NT], wdq_v[:, kk, oc * 128:(oc + 1) * 128], hb[:, kk, 0:NT], kk == 0, kk == NCH - 1) for kk in range(NCH)],
                     reads=[wm_c, hc], writes=[PS.cells[b]])
            rstd, rstd_c = norm_stats(lambda c: PS.t[qb[c]][:, 0:NT], [PS.cells[q] for q in qb], 3, 128, QL, NT)
            for oc in range(3):
                P.op("dve", lambda e, oc=oc, rstd=rstd: e.scalar_tensor_tensor(out=qlat[:, oc, 0:NT], in0=PS.t[qb[oc]][:, 0:NT], scalar=gcol(G_QL + i * 3 + oc),
                                                                    in1=rstd[:, 0:NT], op0=ALU.mult, op1=ALU.mult),
                     reads=[PS.cells[qb[oc]], rstd_c, const_c], writes=[qlat_c])
                PS.release(qb[oc])
            def qproj(h):
                bn = PS.alloc()
                P.op("pe", [pe_mm(PS.t[bn][:, 0:NT], wuq_v[:, kk, h * 192:h * 192 + 128], qlat[:, kk, 0:NT], kk == 0, kk == 2) for kk in range(3)],
                     reads=[wm_c, qlat_c], writes=[PS.cells[bn]])
                br = PS.alloc()
                P.op("pe", [pe_mm(PS.t[br][0:64, 0:NT], wuq_v[:, kk, h * 192 + 128:h * 192 + 192], qlat[:, kk, 0:NT], kk == 0, kk == 2) for kk in range(3)],
                     reads=[wm_c, qlat_c], writes=[PS.cells[br]])
                return bn, br

            def q_s2(h, bn, br):
                b_ = rope_s2a(br, NT)
                rstd, rstd_c = norm_stats(lambda c, bn=bn: PS.t[bn][:, 0:NT], [PS.cells[bn]], 1, 128, 128, NT)
                P.op("dve", lambda e, bn=bn, h=h, rstd=rstd: e.scalar_tensor_tensor(out=qn[:, h, 0:NT], in0=PS.t[bn][:, 0:NT], scalar=gcol(G_QN + i), in1=rstd[:, 0:NT],
                                                                                    op0=ALU.mult, op1=ALU.mult),
                     reads=[PS.cells[bn], rstd_c, const_c], writes=[qn_c[h]])
                PS.release(bn)
                rope_s2b(br, b_, G_QR + i, NT, h % 2)
                PS.release(br)

            pj = {0: qproj(0), 1: qproj(1)}
            q_s2(0, *pj[0])
            for h in range(NH):
                if h + 2 < NH:
                    pj[h + 2] = qproj(h + 2)
                if h + 1 < NH:
                    q_s2(h + 1, *pj[h + 1])
                rope_s3(NT, h % 2, qr[:, h, 0:NT], qr_c[h])
            oT = hb
            for h in range(NH if 'core' not in SKIP else 0):
                s = h % 2
                if h + 1 < NH:
                    issue_kv_load(h + 1, nk)
                O = PS.alloc()
                Dn = PS.alloc()
                blocks = []
                for kb in range(nkb):
                    k0 = kb * 128
                    kn = min(128, nk - k0)
                    c0 = 0 if k0 < pos0 else (k0 - pos0)
                    if c0 >= NT:
                        continue
                    blocks.append((kb, k0, kn, c0))
                pend = None
                first = True
                for bi, (kb, k0, kn, c0) in enumerate(blocks):
                    S = PS.alloc()
                    ncol = NT - c0
                    P.op("pe", [pe_mm(PS.t[S][0:kn, 0:ncol], kvk[s][:, k0:k0 + kn], qn[:, h, c0:NT], True, False),
                                pe_mm(PS.t[S][0:kn, 0:ncol], Kr[:, k0:k0 + kn], qr[:, h, c0:NT], False, True)],
                         reads=[kv_c[s], kr_c, qn_c[h], qr_c[h]], writes=[PS.cells[S]])
                    ps_ = pT_n[0] % 4
                    pT_n[0] += 1
                    P.op("act", lambda e, S=S, ps_=ps_, kn=kn, ncol=ncol: e.activation(out=pT[ps_][0:kn, 0:ncol], in_=PS.t[S][0:kn, 0:ncol], func=AF.Exp, scale=ATTN_SCALE),
                         reads=[PS.cells[S]], writes=[pT_c[ps_]])
                    PS.release(S)
                    if k0 >= pos0 and kn > 64:
                        P.op("dve", lambda e, ps_=ps_, kn=kn: e.memset(pT[ps_][64:kn, 0:64], 0.0), writes=[pT_c[ps_]])
                    cur = (ps_, kb, kn, c0, ncol)
                    if pend is not None:
                        emit_ov(pend, O, Dn, s, first, False)
                        first = False
                    pend = cur
                emit_ov(pend, O, Dn, s, first, True)
                P.op("dve", lambda e, Dn=Dn: e.reciprocal(out=rden[:, 0:NT], in_=PS.t[Dn][:, 0:NT]), reads=[PS.cells[Dn]], writes=[rden_c])
                P.op("dve", lambda e, O=O, h=h: e.tensor_tensor(out=oT[:, h, 0:NT], in0=PS.t[O][:, 0:NT], in1=rden[:, 0:NT], op=ALU.mult),
                     reads=[PS.cells[O], rden_c], writes=[hc[h]])
                PS.release(O)
                PS.release(Dn)
            for dc in range(NCH):
                Y = PS.alloc()
                P.op("pe", [pe_mm(PS.t[Y][:, 0:NT], wo_v[:, h, dc * 128:(dc + 1) * 128], oT[:, h, 0:NT], h == 0, h == NH - 1) for h in range(NH)],
                     reads=[wm_c, hc], writes=[PS.cells[Y]])
                P.op("dve", lambda e, Y=Y, dc=dc: e.tensor_tensor(out=xT[:, dc, 0:NT], in0=PS.t[Y][:, 0:NT], in1=xT[:, dc, 0:NT], op=ALU.add),
                     reads=[PS.cells[Y], xc[dc]], writes=[xc[dc]])
                PS.release(Y)
                x_square(dc, NT)
            next_wm()

        def emit_ov(pend, O, Dn, s, first, last):
            ps_, kb, kn, c0, ncol = pend
            NTl = c0 + ncol
            P.op("pe", [pe_mm(PS.t[O][:, c0:NTl], kvv[s][0:kn, kb, :], pT[ps_][0:kn, 0:ncol], first, last),
                        pe_mm(PS.t[Dn][:, c0:NTl], ones[0:kn, :], pT[ps_][0:kn, 0:ncol], first, last)],
                 reads=[kv_c[s], pT_c[ps_], ones_c], writes=[PS.cells[O], PS.cells[Dn]])

        next_wm(now=True)
        def tparams(ti_):
            kind, sidx, t = tiles[ti_]
            if kind == "p":
                NT = TT
                pos0 = t * TT
                row0 = sidx * SEQ + pos0
                return dict(kind=kind, sidx=sidx, t=t, NT=NT, pos0=pos0, xsrc=xp[row0:row0 + NT, :], ydst=yp[row0:row0 + NT, :],
                            cdst=ckvp[row0:row0 + NT, :], kdst=krp[row0:row0 + NT, :], first_tile=(t == 0), last_tile=(t == TPS - 1))
            return dict(kind=kind, sidx=sidx, t=t, NT=SNT, pos0=PAST, xsrc=xs, ydst=ys, cdst=ckvs, kdst=krs, first_tile=False, last_tile=True)

        for ti in range(len(tiles)):
            tp_ = tparams(ti)
            kind, sidx, t, NT, pos0 = tp_["kind"], tp_["sidx"], tp_["t"], tp_["NT"], tp_["pos0"]
            xsrc, ydst, cdst, kdst = tp_["xsrc"], tp_["ydst"], tp_["cdst"], tp_["kdst"]
            first_tile, last_tile = tp_["first_tile"], tp_["last_tile"]
            load_x(ti, xsrc, NT)
            P.op("sp", [lambda e, pos0=pos0, NT=NT: e.dma_start(out=cosS[:, 0:NT], in_=cos_d[:, pos0:pos0 + NT]),
                        lambda e, pos0=pos0, NT=NT: e.dma_start(out=sinS[:, 0:NT], in_=sin_d[:, pos0:pos0 + NT])],
                 writes=[cs_c], sem=cs_sem, inc=16)
            if kind == "p" and t == 0:
                for L in range(N_A):
                    P.op("dve", lambda e, L=L: e.memset(st[L][:], 0.0), writes=[st_c[L]])
            if kind == "s":
                for L in range(N_A):
                    load_state(L, spool[L])
            for L in range(DEPTH):
                if L >= MAXL:
                    break
                if L == N_A and "latent" not in SKIP:
                    if kind == "s":
                        sample_past()
                    latent(NT, pos0, cdst, kdst)
                if "ffn" not in SKIP:
                    ffn(L, 0, NT)
                if L < N_A:
                    if "pool" not in SKIP:
                        pool_mixer(L, NT, first_tile)
                        if last_tile:
                            store_state(L, npp[L, sidx] if kind == "p" else nps[L])
                elif "attn" not in SKIP:
                    attention(L - N_A, L, NT, pos0)
                if "ffn2" not in SKIP and "ffn" not in SKIP:
                    if L == DEPTH - 1 and ti + 1 < len(tiles):
                        nx = tparams(ti + 1)
                        for blk_ in range(min(2, (nx["NT"] + 127) // 128)):
                            issue_x(ti + 1, nx["xsrc"], nx["NT"], blk_)
                    ffn(L, 1, NT)
            store_y(ydst, NT)

        for tk in out_tickets:
            P.wait_ticket("sp", tk)

        @block.tensor
        def _(e):
            P.replay("pe", e)

        @block.scalar
        def _(e):
            P.replay("act", e)

        @block.vector
        def _(e):
            P.replay("dve", e)

        @block.gpsimd
        def _(e):
            P.replay("pool", e)

        @block.sync
        def _(e):
            P.replay("sp", e)
    return nc


def _consts():
    ident = np.eye(128, dtype=np.float32)
    rrot = np.zeros((64, 64), np.float32)
    for m in range(32):
        rrot[m + 32, m] = -1.0
        rrot[m, m + 32] = 1.0
    inv = (10000.0 ** (-(np.arange(0, 64, 2, dtype=np.float32) / np.float32(64)))).astype(np.float32)
    pos = np.arange(2048, dtype=np.float32)
    ang = (pos[None, :] * inv[:, None]).astype(np.float32)
    cosT = np.concatenate([np.cos(ang), np.cos(ang)], 0).astype(np.float32)
    sinT = np.concatenate([np.sin(ang), np.sin(ang)], 0).astype(np.float32)
    invc = np.zeros((128, 64), np.float32)
    for g, w in enumerate((2, 4, 8, 16)):
        invc[:, g * 16:(g + 1) * 16] = 1.0 / np.minimum(np.arange(16) + 1, w).astype(np.float32)
    return ident, rrot, cosT, sinT, invc


def _gains(inp):
    g = np.zeros((128, NG), np.float32)

    def put(col, vec):
        v = np.asarray(vec, np.float32).reshape(-1)
        if v.size >= 128:
            n = v.size // 128
            g[:, col:col + n] = v.reshape(n, 128).T
        else:
            g[0:v.size, col] = v
    for L in range(DEPTH):
        put(G_FFN1 + L * 8, inp["ffn1_norm"][L])
        put(G_MIX + L * 8, inp["mix_norm"][L])
        put(G_FFN2 + L * 8, inp["ffn2_norm"][L])
    for L in range(N_A):
        put(G_PSC + L * 8, inp["pool_scale"][L])
    put(G_KV, inp["kv_norm"])
    put(G_C, inp["c_norm"])
    for i in range(2):
        put(G_QL + i * 3, inp["q_lat_norm"][i])
        put(G_QN + i, inp["qn_norm"][i])
        put(G_QR + i, inp["qr_norm"][i])
    put(G_KN, inp["kn_norm"])
    put(G_KR, inp["kr_norm"])
    return g


def run(inp, n_cores, NSEQ, SEQ, trace=False, with_sample=True):
    nc = build(NSEQ, SEQ, with_sample=with_sample)
    ident, rrot, cosT, sinT, invc = _consts()
    gains = _gains(inp)
    f = lambda a: np.ascontiguousarray(np.asarray(a, np.float32))
    shared = {
        "w1in": f(inp["ffn1_w_in"]), "w1out": f(inp["ffn1_w_out"]), "w2in": f(inp["ffn2_w_in"]), "w2out": f(inp["ffn2_w_out"]),
        "poolw": f(inp["pool_w"]), "wdkv": f(inp["w_dkv"]), "wkr": f(inp["w_kr"]), "wuk": f(inp["w_uk"]), "wuv": f(inp["w_uv"]),
        "wdq": f(inp["w_dq"]), "wuq": f(inp["w_uq"]), "wo": f(inp["w_o"]),
        "gains": gains, "ident": ident, "rrot": rrot, "cosT": cosT, "sinT": sinT, "invcnt": invc,
    }
    xpr = f(inp["x_prompt"])
    in_maps = []
    for c in range(n_cores):
        m = dict(shared)
        m["xp"] = np.ascontiguousarray(xpr[c * NSEQ:(c + 1) * NSEQ].reshape(NSEQ * SEQ, D))
        m["xs"] = f(inp["x_sample"][c])
        m["spool"] = f(inp["state_pool"][:, c])
        m["cckv"] = f(inp["cache_ckv"][c])
        m["ckr"] = f(inp["cache_krope"][c])
        in_maps.append(m)
    res = run_bass_kernel_spmd(nc, in_maps, core_ids=list(range(n_cores)), trace=trace)
    R = res.results
    y_prompt = np.concatenate([r["yp"].reshape(NSEQ, SEQ, D) for r in R], 0)
    y_sample = np.stack([r["ys"] for r in R], 0)
    npp = np.concatenate([r["npp"] for r in R], 1)
    nps = np.stack([r["nps"] for r in R], 1)
    ckvp = np.concatenate([r["ckvp"].reshape(NSEQ, SEQ, KVL) for r in R], 0)
    krp = np.concatenate([r["krp"].reshape(NSEQ, SEQ, ROPE) for r in R], 0)
    ckvs = np.stack([r["ckvs"] for r in R], 0)
    krs = np.stack([r["krs"] for r in R], 0)
    outs = (y_prompt, y_sample, npp, nps, ckvp, krp, ckvs, krs)
    return tuple(np.ascontiguousarray(o.astype(np.float32)) for o in outs), res


def kernel(**inputs):
    outs, _ = run(inputs, N_CORES, 4, 2048)
    return outs
```
